# Optimizing a Trainium2 kernel written in Bass

```python
import jax, jax.numpy as jnp
from jax import lax
import numpy as np

D_MODEL = 2048
BATCH = 1
SEQ = 8192
DEPTH = 4

N_MIXERS = 2
GRID_W = 64
N_HEADS = 16
HEAD_DIM = D_MODEL // N_HEADS
WIN_H_MAX = 8
WIN_W = 16
CONV_WIDTH = 31
D_FF = 4 * D_MODEL
N_CONV_LAYERS = (DEPTH + N_MIXERS - 1) // N_MIXERS
N_NA_LAYERS = DEPTH // N_MIXERS
NORM_EPS = 1e-5

kernel_name = "hybrid_conformer_natten_encoder"


def rms_norm(x, g):
    xf = x.astype(jnp.float32)
    y = xf * lax.rsqrt(jnp.mean(xf * xf, axis=-1, keepdims=True) + NORM_EPS)
    return (y * g.astype(jnp.float32)).astype(x.dtype)


def layer_norm(x, g, b):
    xf = x.astype(jnp.float32)
    mu = jnp.mean(xf, axis=-1, keepdims=True)
    xc = xf - mu
    y = xc * lax.rsqrt(jnp.mean(xc * xc, axis=-1, keepdims=True) + NORM_EPS)
    return (y * g.astype(jnp.float32) + b.astype(jnp.float32)).astype(x.dtype)


def conformer_conv(h, w_pw1, b_pw1, w_dw, b_dw, ln_g, ln_b, w_pw2, b_pw2):
    u = h @ w_pw1 + b_pw1
    a, gate = jnp.split(u, 2, axis=-1)
    u = a * jax.nn.sigmoid(gate)
    u = lax.conv_general_dilated(
        u, w_dw[:, None, :], window_strides=(1,),
        padding=[(CONV_WIDTH // 2, CONV_WIDTH // 2)],
        dimension_numbers=("NWC", "WIO", "NWC"),
        feature_group_count=D_MODEL) + b_dw
    u = jax.nn.silu(layer_norm(u, ln_g, ln_b))
    return u @ w_pw2 + b_pw2


def neighbourhood_attention(h, w_qkv, b_qkv, rpb, w_o, b_o):
    bsz, t, _ = h.shape
    rows = t // GRID_W
    kh = min(WIN_H_MAX, rows)
    qkv = (h @ w_qkv + b_qkv).reshape(bsz, rows, GRID_W, 3, N_HEADS, HEAD_DIM)
    q = qkv[:, :, :, 0] * (HEAD_DIM ** -0.5)
    k = qkv[:, :, :, 1]
    v = qkv[:, :, :, 2]
    row_ids = jnp.arange(rows, dtype=jnp.int32)
    row_start = jnp.clip(row_ids - kh // 2, 0, rows - kh)
    col_ids = jnp.arange(GRID_W, dtype=jnp.int32)
    col_start = jnp.clip(col_ids - WIN_W // 2, 0, GRID_W - WIN_W)
    col_idx = col_start[:, None] + jnp.arange(WIN_W, dtype=jnp.int32)[None, :]
    col_bias_idx = col_idx - col_ids[:, None] + (WIN_W - 1)
    rpb_cols = rpb[:, :, col_bias_idx]

    def row_block(r):
        rs = row_start[r]
        q_r = lax.dynamic_index_in_dim(q, r, axis=1, keepdims=False)
        k_band = lax.dynamic_slice_in_dim(k, rs, kh, axis=1)
        v_band = lax.dynamic_slice_in_dim(v, rs, kh, axis=1)
        k_win = k_band[:, :, col_idx]
        v_win = v_band[:, :, col_idx]
        row_bias_idx = rs + jnp.arange(kh, dtype=jnp.int32) - r + (WIN_H_MAX - 1)
        bias = jnp.transpose(rpb_cols[:, row_bias_idx], (0, 2, 1, 3))
        s = jnp.einsum("bqhd,biqjhd->bhqij", q_r, k_win) + bias[None]
        p = jax.nn.softmax(
            s.astype(jnp.float32).reshape(bsz, N_HEADS, GRID_W, kh * WIN_W), axis=-1)
        p = p.reshape(s.shape).astype(v.dtype)
        return jnp.einsum("bhqij,biqjhd->bqhd", p, v_win)

    o = lax.map(row_block, row_ids)
    o = jnp.moveaxis(o, 0, 1).reshape(bsz, t, D_MODEL)
    return o @ w_o + b_o


def sq_relu_mlp(h, w_up, w_down):
    return jnp.square(jax.nn.relu(h @ w_up)) @ w_down


def setup_inputs(seed: int = 0) -> dict:
    key = jax.random.key(seed)
    ks = jax.random.split(key, 24)

    def nrm(k, shape, scale):
        return jax.random.normal(k, shape, jnp.float32) * scale

    d = D_MODEL
    return {
        "x": nrm(ks[0], (BATCH, SEQ, d), 1.0),
        "norm_mix_g": 1.0 + nrm(ks[1], (DEPTH, d), 0.02),
        "norm_ffn_g": 1.0 + nrm(ks[2], (DEPTH, d), 0.02),
        "final_norm_g": 1.0 + nrm(ks[3], (d,), 0.02),
        "conv_w_pw1": nrm(ks[4], (N_CONV_LAYERS, d, 2 * d), d ** -0.5),
        "conv_b_pw1": nrm(ks[5], (N_CONV_LAYERS, 2 * d), 0.02),
        "conv_w_dw": nrm(ks[6], (N_CONV_LAYERS, CONV_WIDTH, d), CONV_WIDTH ** -0.5),
        "conv_b_dw": nrm(ks[7], (N_CONV_LAYERS, d), 0.02),
        "conv_ln_g": 1.0 + nrm(ks[8], (N_CONV_LAYERS, d), 0.02),
        "conv_ln_b": nrm(ks[9], (N_CONV_LAYERS, d), 0.02),
        "conv_w_pw2": nrm(ks[10], (N_CONV_LAYERS, d, d), d ** -0.5),
        "conv_b_pw2": nrm(ks[11], (N_CONV_LAYERS, d), 0.02),
        "na_w_qkv": nrm(ks[12], (N_NA_LAYERS, d, 3 * d), d ** -0.5),
        "na_b_qkv": nrm(ks[13], (N_NA_LAYERS, 3 * d), 0.02),
        "na_rpb": nrm(ks[14], (N_NA_LAYERS, N_HEADS, 2 * WIN_H_MAX - 1, 2 * WIN_W - 1), 0.1),
        "na_w_o": nrm(ks[15], (N_NA_LAYERS, d, d), d ** -0.5),
        "na_b_o": nrm(ks[16], (N_NA_LAYERS, d), 0.02),
        "ffn_w_up": nrm(ks[17], (DEPTH, d, D_FF), d ** -0.5),
        "ffn_w_down": nrm(ks[18], (DEPTH, D_FF, d), D_FF ** -0.5),
    }


def reference(x, norm_mix_g, norm_ffn_g, final_norm_g,
              conv_w_pw1, conv_b_pw1, conv_w_dw, conv_b_dw, conv_ln_g, conv_ln_b,
              conv_w_pw2, conv_b_pw2,
              na_w_qkv, na_b_qkv, na_rpb, na_w_o, na_b_o,
              ffn_w_up, ffn_w_down):
    h = x
    for i in range(DEPTH):
        j = i // N_MIXERS
        hn = rms_norm(h, norm_mix_g[i])
        if i % N_MIXERS == 0:
            h = h + conformer_conv(hn, conv_w_pw1[j], conv_b_pw1[j], conv_w_dw[j],
                                   conv_b_dw[j], conv_ln_g[j], conv_ln_b[j],
                                   conv_w_pw2[j], conv_b_pw2[j])
        else:
            h = h + neighbourhood_attention(hn, na_w_qkv[j], na_b_qkv[j], na_rpb[j],
                                            na_w_o[j], na_b_o[j])
        h = h + sq_relu_mlp(rms_norm(h, norm_ffn_g[i]), ffn_w_up[i], ffn_w_down[i])
    return rms_norm(h, final_norm_g)
```

```python
import numpy as np
import concourse.bass as bass
import concourse.mybir as mybir
from concourse.bass_utils import run_bass_kernel_spmd

F32 = mybir.dt.float32
BF16 = mybir.dt.bfloat16
AF = mybir.ActivationFunctionType
ALU = mybir.AluOpType

D = 2048
KC = 16
TOK = 1024
NCORE = 8
EPS = 1e-5
NEG = -30000.0
CW = 31
VAR_OF_BLOCK = [1, 2, 0, 0, 0, 0, 3, 4]


class Res:
    __slots__ = ("name", "w", "readers")

    def __init__(self, name):
        self.name = name
        self.w = None
        self.readers = []


class Planner:
    ENGS = ("pe", "act", "dve", "pool", "sp")

    def __init__(self, nc):
        self.nc = nc
        self.ops = {k: [] for k in self.ENGS}
        self.cnt = {k: 0 for k in self.ENGS}
        self.sem = {}
        self.seen = {k: {} for k in self.ENGS}
        self.dma_sems = {}
        self._keep = []
        for k in self.ENGS:
            self.sem[k] = self._new_sem("c_" + k)

    def _new_sem(self, name):
        cm = self.nc.semaphore(name)
        h = cm.__enter__()
        self._keep.append(cm)
        return h

    def _collect(self, eng, reads, writes, is_dma):
        need = {}

        def add(c, same_ok):
            if c is None:
                return
            ek, sem, val = c
            if ek == eng and same_ok and not is_dma:
                return
            key = id(sem)
            if key not in need or need[key][1] < val:
                need[key] = (sem, val)

        for r in reads:
            add(r.w, False)
        for w in writes:
            add(w.w, True)
            for rd in w.readers:
                add(rd, True)
        out = []
        seen = self.seen[eng]
        for key, (sem, val) in need.items():
            if seen.get(key, 0) >= val:
                continue
            seen[key] = val
            out.append((sem, val))
        return out

    def _commit(self, comp, reads, writes):
        for r in reads:
            r.readers.append(comp)
            if len(r.readers) > 24:
                best = {}
                for c in r.readers:
                    k = id(c[1])
                    if k not in best or best[k][2] < c[2]:
                        best[k] = c
                r.readers = list(best.values())
        for w in writes:
            w.w = comp
            w.readers = []

    def op(self, eng, fn, reads=(), writes=()):
        waits = self._collect(eng, reads, writes, False)
        self.cnt[eng] += 1
        comp = (eng, self.sem[eng], self.cnt[eng])
        self.ops[eng].append((waits, fn, (self.sem[eng], 1)))
        self._commit(comp, reads, writes)
        return comp

    def dma(self, eng, fn, slot, reads=(), writes=()):
        waits = self._collect(eng, reads, writes, True)
        if slot not in self.dma_sems:
            self.dma_sems[slot] = [self._new_sem("d_" + slot), 0]
        ent = self.dma_sems[slot]
        ent[1] += 16
        comp = ("dma:" + slot, ent[0], ent[1])
        self.ops[eng].append((waits, fn, (ent[0], 16)))
        self._commit(comp, reads, writes)
        return comp

    def final_wait(self, eng, slots):
        waits = [(self.dma_sems[s][0], self.dma_sems[s][1]) for s in slots]
        self.ops[eng].append((waits, None, None))

    def emit(self):
        ops = self.ops

        def run(e, lst):
            for waits, fn, inc in lst:
                for sem, val in waits:
                    e.wait_ge(sem, val)
                if fn is not None:
                    fn(e).then_inc(inc[0], inc[1])

        with self.nc.Block() as block:
            @block.tensor
            def _(e):
                run(e, ops["pe"])

            @block.scalar
            def _(e):
                run(e, ops["act"])

            @block.vector
            def _(e):
                run(e, ops["dve"])

            @block.gpsimd
            def _(e):
                run(e, ops["pool"])

            @block.sync
            def _(e):
                run(e, ops["sp"])


class Ctx:
    pass


def build_layer(kind, last):
    nc = bass.Bass("TRN2", target_bir_lowering=False)
    P = Planner(nc)
    NB = 1056 if kind == "conv" else 1536
    OFF = 16 if kind == "conv" else 256

    def din(name, shape, dt=F32):
        return nc.dram_tensor(name, list(shape), dt, kind="ExternalInput").ap()

    hband = din("hband", [128, KC, NB])
    vecs = din("vecs", [128, 16, KC])
    identd = din("ident", [128, 128])
    w_up = din("w_up", [32, 128, KC, 256])
    w_dn = din("w_dn", [32, 128, KC, 256])
    if kind == "conv":
        w_pw1 = din("w_pw1", [16, 128, KC, 256])
        w_pw2 = din("w_pw2", [8, 128, KC, 256])
        wdw = din("wdw", [128, KC, CW])
        hmask = din("hmask", [128, 2, KC, 16])
    else:
        w_qkv = din("w_qkv", [16, 128, KC, 384])
        w_o = din("w_o", [8, 128, KC, 256])
        nab = din("nab", [16, 128, 5, 640])
    hout = nc.dram_tensor("hout", [128, KC, TOK], F32, kind="ExternalOutput").ap()

    V_GMIX, V_GFFN, V_GFIN = 0, 1, 2
    V_BA, V_BG, V_BDW, V_LNG, V_LNB, V_BPW2 = 3, 4, 5, 6, 7, 8
    V_BQ, V_BK, V_BV, V_BO = 3, 4, 5, 6

    SB_BASE = 16576
    off = [SB_BASE]

    def alloc(name, shape, dt, at=None):
        nbytes = int(np.prod(shape[1:])) * (4 if dt == F32 else 2)
        if at is None:
            at = off[0]
            off[0] = at + ((nbytes + 63) // 64) * 64
        return nc.alloc_sbuf_tensor_at(name, list(shape), dt, offset=at), at

    RINGW = 384 if kind == "na" else 256
    NSLOT = 4
    ring = [alloc(f"ring{i}", [128, KC, RINGW], BF16)[0] for i in range(NSLOT)]
    r_ring = [Res(f"ring{i}") for i in range(NSLOT)]
    vt, _ = alloc("vecs_sb", [128, 16, KC], F32)
    ident, _ = alloc("ident_sb", [128, 128], BF16)
    onesD, _ = alloc("onesD", [128, 128], BF16)
    ones1, _ = alloc("ones1", [128, 128], BF16)
    epst, _ = alloc("epst", [128, 16], F32)
    sq = [alloc(f"sq{i}", [128, 512], BF16)[0] for i in range(2)]
    r_sq = [Res(f"sq{i}") for i in range(2)]
    rstd_t = [alloc(f"rstd{i}", [128, 512], F32)[0] for i in range(2)]
    r_rstd = [Res(f"rstd{i}") for i in range(2)]
    tmpa = [alloc(f"tmpa{i}", [128, 512], F32)[0] for i in range(2)]
    r_tmpa = [Res(f"tmpa{i}") for i in range(2)]
    if kind == "conv":
        tmpb = [alloc(f"tmpb{i}", [128, 512], F32)[0] for i in range(2)]
        r_tmpb = [Res(f"tmpb{i}") for i in range(2)]

    r_vt, r_ident, r_ones = Res("vt"), Res("ident"), Res("ones")

    if kind == "conv":
        H, _ = alloc("H", [128, KC, TOK], F32)
        A, _ = alloc("A", [128, KC, NB], BF16)
        B, _ = alloc("B", [128, KC, NB], BF16)
        stg, _ = alloc("stg", [128, KC, 32], F32)
        hh, _ = alloc("hh", [128, KC, 32], BF16)
        r_hh = Res("hh")
        wdw_sb, _ = alloc("wdw_sb", [128, KC, CW], F32)
        hm_sb, _ = alloc("hm_sb", [128, 2, KC, 16], F32)
        dg = [alloc(f"dg{i}", [128, CW, 128], BF16)[0] for i in range(2)]
        r_dg = [Res(f"dg{i}") for i in range(2)]
        mean_sb = [alloc(f"mean{i}", [128, 512], F32)[0] for i in range(2)]
        r_mean = [Res(f"mean{i}") for i in range(2)]
    else:
        A, a_at = alloc("A", [128, KC, NB], BF16)
        QT = [alloc(f"QT{i}", [128, TOK], BF16)[0] for i in range(2)]
        KT = [alloc(f"KT{i}", [128, NB], BF16)[0] for i in range(2)]
        VT = [alloc(f"VT{i}", [128, NB], BF16)[0] for i in range(2)]
        Vh = [alloc(f"Vh{i}", [128, 12, 128], BF16)[0] for i in range(2)]
        nb_sb = [alloc(f"nab{i}", [128, 5, 640], F32)[0] for i in range(2)]
        s_sb = [alloc(f"s_sb{i}", [128, 640], F32)[0] for i in range(2)]
        pt_sb = [alloc(f"pt_sb{i}", [128, 640], BF16)[0] for i in range(2)]
        rc_sb = [alloc(f"rc_sb{i}", [128, 128], F32)[0] for i in range(2)]
        stg, _ = alloc("stg", [128, KC, 128], F32)
        att_end = off[0]
        B, _ = alloc("B", [128, KC, TOK], BF16)
        H = nc.alloc_sbuf_tensor_at("H", [128, KC, TOK], F32, offset=a_at)
        A2 = nc.alloc_sbuf_tensor_at("A2", [128, KC, TOK], BF16, offset=a_at + KC * TOK * 4)
        assert a_at + KC * TOK * 4 + KC * TOK * 2 <= att_end, (a_at, att_end)
        r_QT = [Res(f"QT{i}") for i in range(2)]
        r_KT = [Res(f"KT{i}") for i in range(2)]
        r_VT = [Res(f"VT{i}") for i in range(2)]
        r_Vh = [Res(f"Vh{i}") for i in range(2)]
        r_nb = [Res(f"nab{i}") for i in range(2)]
        r_s = [Res(f"s_sb{i}") for i in range(2)]
        r_pt = [Res(f"pt_sb{i}") for i in range(2)]
        r_rc = [Res(f"rc_sb{i}") for i in range(2)]
    assert off[0] <= 229344, off[0]

    r_stg = Res("stg")
    r_H = [Res(f"H{j}") for j in range(KC)]
    r_A = [Res(f"A{j}") for j in range(KC)]
    r_B = [Res(f"B{j}") for j in range(KC)]

    ps = [nc.alloc_psum_tensor(f"ps{i}", [128, 512], F32) for i in range(8)]
    r_ps = [Res(f"ps{i}") for i in range(8)]
    gb = [0]

    def next_bank(nb=3):
        b = gb[0] % nb
        gb[0] += 1
        return b

    P.dma("sp", lambda e: e.dma_start(out=vt[:], in_=vecs), "vt", writes=[r_vt])
    P.dma("pool", lambda e: e.dma_start(out=ident[:], in_=identd), "ident", writes=[r_ident])
    P.op("dve", lambda e: e.memset(onesD[:], 1.0 / D), writes=[r_ones])
    P.op("dve", lambda e: e.memset(ones1[:], 1.0), writes=[r_ones])

    def vec(slot, j):
        return vt[:, slot, j:j + 1]

    P.op("dve", lambda e: e.memset(epst[:], EPS), writes=[r_ones])

    def rsqrt_to(i, src_ap, src_res, n):
        P.op("act", lambda e: e.activation(out=rstd_t[i][:, 0:n], in_=src_ap, func=AF.Sqrt, bias=epst[:, 0:1]),
             reads=src_res + [r_ones], writes=[r_rstd[i]])
        P.op("dve", lambda e: e.reciprocal(out=rstd_t[i][:, 0:n], in_=rstd_t[i][:, 0:n]),
             reads=[r_rstd[i]], writes=[r_rstd[i]])

    pc = [0]

    def load_piece(wd, p, fw):
        s = pc[0] % NSLOT
        pc[0] += 1
        P.dma("pool", lambda e: e.dma_start(out=ring[s][:, :, 0:fw], in_=wd[p]), f"ring{s}",
              writes=[r_ring[s]])
        return s

    def mm_group(bank, n, lhs_fn, rhs_fn, nk, reads):
        def f(e):
            ins = None
            for k in range(nk):
                ins = e.matmul(ps[bank][:, 0:n], lhs_fn(k), rhs_fn(k), start=(k == 0), stop=(k == nk - 1))
            return ins
        P.op("pe", f, reads=reads, writes=[r_ps[bank]])

    nrm = [0]

    def rms_tile(src_fn, src_res, n, gslot, dst_fn, dst_res_fn):
        i = nrm[0] % 2
        nrm[0] += 1
        bank = 3 + (nrm[0] % 2)
        for j in range(KC):
            q = (nrm[0] * KC + j) % 2
            P.op("act", lambda e, j=j, q=q: e.activation(out=sq[q][:, 0:n], in_=src_fn(j), func=AF.Square),
                 reads=src_res(j), writes=[r_sq[q]])
            P.op("pe", lambda e, j=j, q=q: e.matmul(ps[bank][:, 0:n], onesD[:], sq[q][:, 0:n],
                                                    start=(j == 0), stop=(j == KC - 1)),
                 reads=[r_sq[q], r_ones], writes=[r_ps[bank]] if j == 0 else [])
        r_ps[bank].w = ("pe", P.sem["pe"], P.cnt["pe"])
        rsqrt_to(i, ps[bank][:, 0:n], [r_ps[bank]], n)
        for j in range(KC):
            P.op("dve", lambda e, j=j: e.scalar_tensor_tensor(out=dst_fn(j), in0=src_fn(j), scalar=vec(gslot, j),
                                                              in1=rstd_t[i][:, 0:n], op0=ALU.mult, op1=ALU.mult),
                 reads=src_res(j) + [r_rstd[i], r_vt], writes=dst_res_fn(j))

    def mlp(HN, r_HN, UP, r_UP):
        for tt in range(2):
            rms_tile(lambda j, tt=tt: H[:, j, tt * 512:(tt + 1) * 512], lambda j: [r_H[j]], 512, V_GFFN,
                     lambda j, tt=tt: HN[:, j, tt * 512:(tt + 1) * 512], lambda j: [r_HN[j]])
        for blk in range(4):
            for pp in range(8):
                s = load_piece(w_up, blk * 8 + pp, 256)
                for fc in range(2):
                    c = 2 * pp + fc
                    for tt in range(2):
                        bank = next_bank()
                        mm_group(bank, 512, lambda k, s=s, fc=fc: ring[s][:, k, fc * 128:(fc + 1) * 128],
                                 lambda k, tt=tt: HN[:, k, tt * 512:(tt + 1) * 512], KC,
                                 [r_ring[s]] + r_HN)
                        q = gb[0] % 2
                        P.op("act", lambda e, bank=bank, q=q: e.activation(out=tmpa[q][:], in_=ps[bank][:], func=AF.Relu),
                             reads=[r_ps[bank]], writes=[r_tmpa[q]])
                        P.op("dve", lambda e, q=q, c=c, tt=tt: e.tensor_tensor(
                            out=UP[:, c, tt * 512:(tt + 1) * 512], in0=tmpa[q][:], in1=tmpa[q][:], op=ALU.mult),
                            reads=[r_tmpa[q]], writes=[r_UP[c]])
            for pp in range(8):
                s = load_piece(w_dn, blk * 8 + pp, 256)
                for fc in range(2):
                    j = 2 * pp + fc
                    for tt in range(2):
                        bank = next_bank()
                        mm_group(bank, 512, lambda k, s=s, fc=fc: ring[s][:, k, fc * 128:(fc + 1) * 128],
                                 lambda k, tt=tt: UP[:, k, tt * 512:(tt + 1) * 512], KC,
                                 [r_ring[s]] + r_UP)
                        P.op("dve", lambda e, bank=bank, j=j, tt=tt: e.tensor_tensor(
                            out=H[:, j, tt * 512:(tt + 1) * 512], in0=ps[bank][:], in1=H[:, j, tt * 512:(tt + 1) * 512],
                            op=ALU.add), reads=[r_ps[bank], r_H[j]], writes=[r_H[j]])

    def store_out():
        if last:
            for tt in range(2):
                i = nrm[0] % 2
                nrm[0] += 1
                bank = 3 + (nrm[0] % 2)
                n = 512
                for j in range(KC):
                    q = j % 2
                    P.op("act", lambda e, j=j, q=q, tt=tt: e.activation(out=sq[q][:], in_=H[:, j, tt * 512:(tt + 1) * 512],
                                                                        func=AF.Square),
                         reads=[r_H[j]], writes=[r_sq[q]])
                    P.op("pe", lambda e, j=j, q=q, bank=bank: e.matmul(ps[bank][:], onesD[:], sq[q][:], start=(j == 0), stop=(j == KC - 1)),
                         reads=[r_sq[q], r_ones], writes=[r_ps[bank]] if j == 0 else [])
                r_ps[bank].w = ("pe", P.sem["pe"], P.cnt["pe"])
                rsqrt_to(i, ps[bank][:], [r_ps[bank]], 512)
                for j in range(KC):
                    P.op("dve", lambda e, j=j, i=i, tt=tt: e.scalar_tensor_tensor(
                        out=H[:, j, tt * 512:(tt + 1) * 512], in0=H[:, j, tt * 512:(tt + 1) * 512], scalar=vec(V_GFIN, j),
                        in1=rstd_t[i][:], op0=ALU.mult, op1=ALU.mult),
                        reads=[r_H[j], r_rstd[i], r_vt], writes=[r_H[j]])
        slots = []
        for g in range(4):
            P.dma("sp", lambda e, g=g: e.dma_start(out=hout[:, 4 * g:4 * g + 4, :], in_=H[:, 4 * g:4 * g + 4, :]),
                  f"out{g}", reads=r_H[4 * g:4 * g + 4])
            slots.append(f"out{g}")
        P.final_wait("sp", slots)

    if kind == "conv":
        for g in range(4):
            P.dma("sp", lambda e, g=g: e.dma_start(out=H[:, 4 * g:4 * g + 4, :], in_=hband[:, 4 * g:4 * g + 4, OFF:OFF + TOK]),
                  f"hin{g}", writes=r_H[4 * g:4 * g + 4])
        P.dma("sp", lambda e: e.dma_start(out=stg[:, :, 0:16], in_=hband[:, :, 0:16]), "stg", writes=[r_stg])
        P.dma("sp", lambda e: e.dma_start(out=stg[:, :, 16:32], in_=hband[:, :, OFF + TOK:NB]), "stg2", writes=[r_stg])
        P.dma("sp", lambda e: e.dma_start(out=wdw_sb[:], in_=wdw), "wdw", writes=[r_vt])
        P.dma("sp", lambda e: e.dma_start(out=hm_sb[:], in_=hmask), "hm", writes=[r_vt])
        for tt in range(2):
            rms_tile(lambda j, tt=tt: H[:, j, tt * 512:(tt + 1) * 512], lambda j: [r_H[j]], 512, V_GMIX,
                     lambda j, tt=tt: A[:, j, OFF + tt * 512:OFF + (tt + 1) * 512], lambda j: [r_A[j]])
        rms_tile(lambda j: stg[:, j, 0:32], lambda j: [r_stg], 32, V_GMIX,
                 lambda j: hh[:, j, :], lambda j: [r_hh])
        for j in range(KC):
            P.op("act", lambda e, j=j: e.activation(out=A[:, j, 0:16], in_=hh[:, j, 0:16], func=AF.Identity),
                 reads=[r_hh], writes=[r_A[j]])
            P.op("act", lambda e, j=j: e.activation(out=A[:, j, OFF + TOK:NB], in_=hh[:, j, 16:32], func=AF.Identity),
                 reads=[r_hh], writes=[r_A[j]])
        ttiles = [(0, 512), (512, 512), (1024, 32)]
        for p in range(16):
            s = load_piece(w_pw1, p, 256)
            for (t0, n) in ttiles:
                banks = []
                for fc in range(2):
                    bank = next_bank()
                    banks.append(bank)
                    mm_group(bank, n, lambda k, s=s, fc=fc: ring[s][:, k, fc * 128:(fc + 1) * 128],
                             lambda k, t0=t0, n=n: A[:, k, t0:t0 + n], KC, [r_ring[s]] + r_A)
                q = gb[0] % 2
                ba, bg = banks
                P.op("act", lambda e, bg=bg, q=q, n=n, p=p: e.activation(out=tmpa[q][:, 0:n], in_=ps[bg][:, 0:n],
                                                                         func=AF.Sigmoid, bias=vec(V_BG, p)),
                     reads=[r_ps[bg], r_vt], writes=[r_tmpa[q]])
                P.op("dve", lambda e, ba=ba, q=q, n=n, p=p, t0=t0: e.scalar_tensor_tensor(
                    out=B[:, p, t0:t0 + n], in0=ps[ba][:, 0:n], scalar=vec(V_BA, p), in1=tmpa[q][:, 0:n],
                    op0=ALU.add, op1=ALU.mult), reads=[r_ps[ba], r_tmpa[q], r_vt], writes=[r_B[p]])
        for j in range(KC):
            P.op("dve", lambda e, j=j: e.tensor_tensor(out=B[:, j, 0:16], in0=B[:, j, 0:16], in1=hm_sb[:, 0, j, :], op=ALU.mult),
                 reads=[r_B[j], r_vt], writes=[r_B[j]])
            P.op("dve", lambda e, j=j: e.tensor_tensor(out=B[:, j, OFF + TOK:NB], in0=B[:, j, OFF + TOK:NB], in1=hm_sb[:, 1, j, :],
                                                       op=ALU.mult), reads=[r_B[j], r_vt], writes=[r_B[j]])
        for j in range(KC):
            di = j % 2
            for k in range(CW):
                P.op("dve", lambda e, j=j, k=k, di=di: e.tensor_scalar(out=dg[di][:, k, :], in0=ident[:], scalar1=wdw_sb[:, j, k:k + 1],
                                                                       scalar2=None, op0=ALU.mult),
                     reads=[r_ident, r_vt], writes=[r_dg[di]])
            for tt in range(2):
                bank = next_bank()
                mm_group(bank, 512, lambda k, di=di: dg[di][:, k, :],
                         lambda k, j=j, tt=tt: B[:, j, tt * 512 + k + 1:tt * 512 + k + 1 + 512], CW, [r_dg[di], r_B[j]])
                P.op("act", lambda e, bank=bank, j=j, tt=tt: e.activation(out=A[:, j, tt * 512:(tt + 1) * 512], in_=ps[bank][:],
                                                                          func=AF.Identity, bias=vec(V_BDW, j)),
                     reads=[r_ps[bank], r_vt], writes=[r_A[j]])
        for tt in range(2):
            i = tt
            bm, bq = 4 + 2 * tt, 5 + 2 * tt
            for j in range(KC):
                q = j % 2
                P.op("act", lambda e, j=j, q=q, tt=tt: e.activation(out=sq[q][:], in_=A[:, j, tt * 512:(tt + 1) * 512], func=AF.Square),
                     reads=[r_A[j]], writes=[r_sq[q]])
                P.op("pe", lambda e, j=j, tt=tt, bm=bm: e.matmul(ps[bm][:], onesD[:], A[:, j, tt * 512:(tt + 1) * 512],
                                                          start=(j == 0), stop=(j == KC - 1)),
                     reads=[r_A[j], r_ones], writes=[r_ps[bm]] if j == 0 else [])
                P.op("pe", lambda e, j=j, q=q, bq=bq: e.matmul(ps[bq][:], onesD[:], sq[q][:], start=(j == 0), stop=(j == KC - 1)),
                     reads=[r_sq[q], r_ones], writes=[r_ps[bq]] if j == 0 else [])
            r_ps[bm].w = ("pe", P.sem["pe"], P.cnt["pe"])
            r_ps[bq].w = ("pe", P.sem["pe"], P.cnt["pe"])
            P.op("act", lambda e, i=i, bm=bm: e.activation(out=mean_sb[i][:], in_=ps[bm][:], func=AF.Identity),
                 reads=[r_ps[bm]], writes=[r_mean[i]])
            P.op("dve", lambda e, i=i: e.tensor_tensor(out=tmpb[i][:], in0=mean_sb[i][:], in1=mean_sb[i][:], op=ALU.mult),
                 reads=[r_mean[i]], writes=[r_tmpb[i]])
            P.op("dve", lambda e, i=i, bq=bq: e.tensor_tensor(out=tmpb[i][:], in0=ps[bq][:], in1=tmpb[i][:], op=ALU.subtract),
                 reads=[r_ps[bq], r_tmpb[i]], writes=[r_tmpb[i]])
            rsqrt_to(i, tmpb[i][:], [r_tmpb[i]], 512)
            for j in range(KC):
                q = j % 2
                P.op("dve", lambda e, j=j, q=q, i=i, tt=tt: e.tensor_tensor(out=tmpa[q][:], in0=A[:, j, tt * 512:(tt + 1) * 512],
                                                                            in1=mean_sb[i][:], op=ALU.subtract),
                     reads=[r_A[j], r_mean[i]], writes=[r_tmpa[q]])
                P.op("dve", lambda e, j=j, q=q, i=i: e.scalar_tensor_tensor(out=tmpa[q][:], in0=tmpa[q][:], scalar=vec(V_LNG, j),
                                                                            in1=rstd_t[i][:], op0=ALU.mult, op1=ALU.mult),
                     reads=[r_tmpa[q], r_rstd[i], r_vt], writes=[r_tmpa[q]])
                P.op("act", lambda e, j=j, q=q, tt=tt: e.activation(out=B[:, j, tt * 512:(tt + 1) * 512], in_=tmpa[q][:],
                                                                    func=AF.Silu, bias=vec(V_LNB, j)),
                     reads=[r_tmpa[q], r_vt], writes=[r_B[j]])
        for p in range(8):
            s = load_piece(w_pw2, p, 256)
            for fc in range(2):
                j = 2 * p + fc
                for tt in range(2):
                    bank = next_bank()
                    mm_group(bank, 512, lambda k, s=s, fc=fc: ring[s][:, k, fc * 128:(fc + 1) * 128],
                             lambda k, tt=tt: B[:, k, tt * 512:(tt + 1) * 512], KC, [r_ring[s]] + r_B)
                    P.op("dve", lambda e, bank=bank, j=j, tt=tt: e.scalar_tensor_tensor(
                        out=H[:, j, tt * 512:(tt + 1) * 512], in0=ps[bank][:], scalar=vec(V_BPW2, j),
                        in1=H[:, j, tt * 512:(tt + 1) * 512], op0=ALU.add, op1=ALU.add),
                        reads=[r_ps[bank], r_H[j], r_vt], writes=[r_H[j]])
        mlp(A, r_A, B, r_B)
        store_out()
    else:
        for t in range(NB // 128):
            P.dma("sp", lambda e, t=t: e.dma_start(out=stg[:], in_=hband[:, :, t * 128:(t + 1) * 128]), "stg",
                  writes=[r_stg])
            rms_tile(lambda j: stg[:, j, :], lambda j: [r_stg], 128, V_GMIX,
                     lambda j, t=t: A[:, j, t * 128:(t + 1) * 128], lambda j: [r_A[j]])
        for h in range(16):
            hb = h % 2
            P.dma("sp", lambda e, h=h, hb=hb: e.dma_start(out=nb_sb[hb][:], in_=nab[h]), f"nab{hb}", writes=[r_nb[hb]])
            s = load_piece(w_qkv, h, 384)
            for tt in range(2):
                bank = next_bank()
                mm_group(bank, 512, lambda k, s=s: ring[s][:, k, 0:128],
                         lambda k, tt=tt: A[:, k, OFF + tt * 512:OFF + (tt + 1) * 512], KC, [r_ring[s]] + r_A)
                P.op("dve", lambda e, bank=bank, hb=hb, h=h, tt=tt: e.tensor_scalar(
                    out=QT[hb][:, tt * 512:(tt + 1) * 512], in0=ps[bank][:], scalar1=vec(V_BQ, h), scalar2=128 ** -0.5,
                    op0=ALU.add, op1=ALU.mult), reads=[r_ps[bank], r_vt], writes=[r_QT[hb]])
            for (fc, dst, r_dst, vs) in ((1, KT, r_KT, V_BK), (2, VT, r_VT, V_BV)):
                for tt in range(3):
                    bank = next_bank()
                    mm_group(bank, 512, lambda k, s=s, fc=fc: ring[s][:, k, fc * 128:(fc + 1) * 128],
                             lambda k, tt=tt: A[:, k, tt * 512:(tt + 1) * 512], KC, [r_ring[s]] + r_A)
                    P.op("act", lambda e, bank=bank, dst=dst, hb=hb, tt=tt, vs=vs, h=h: e.activation(
                        out=dst[hb][:, tt * 512:(tt + 1) * 512], in_=ps[bank][:], func=AF.Identity, bias=vec(vs, h)),
                        reads=[r_ps[bank], r_vt], writes=[r_dst[hb]])
            pT = ps[3][:].bitcast(BF16)
            for g3 in range(3):
                def tr(e, g3=g3, hb=hb):
                    ins = None
                    for i in range(4):
                        kt = g3 * 4 + i
                        ins = e.transpose(pT[:, i * 128:(i + 1) * 128], VT[hb][:, kt * 128:(kt + 1) * 128], ident[:])
                    return ins
                P.op("pe", tr, reads=[r_VT[hb], r_ident], writes=[r_ps[3]])
                P.op("act", lambda e, g3=g3, hb=hb: e.activation(
                    out=Vh[hb][:, 4 * g3:4 * g3 + 4, :].rearrange("p a b -> p (a b)"), in_=pT[:, 0:512], func=AF.Identity),
                    reads=[r_ps[3]], writes=[r_Vh[hb]])
            for b in range(8):
                v = VAR_OF_BLOCK[b]
                ab = (h * 8 + b) % 2
                X, Y = 4 + 2 * ab, 5 + 2 * ab

                def qk(e, b=b, hb=hb, X=X, Y=Y):
                    ins = None
                    for i in range(5):
                        o = ps[X][:, i * 128:(i + 1) * 128] if i < 4 else ps[Y][:, 0:128]
                        ins = e.matmul(o, KT[hb][:, (b + i) * 128:(b + i + 1) * 128], QT[hb][:, b * 128:(b + 1) * 128],
                                       start=True, stop=True)
                    return ins
                P.op("pe", qk, reads=[r_KT[hb], r_QT[hb]], writes=[r_ps[X], r_ps[Y]])
                P.op("dve", lambda e, ab=ab, X=X, hb=hb, v=v: e.scalar_tensor_tensor(
                    out=s_sb[ab][:, 0:512], in0=ps[X][:], scalar=80.0, in1=nb_sb[hb][:, v, 0:512], op0=ALU.min, op1=ALU.add),
                    reads=[r_ps[X], r_nb[hb]], writes=[r_s[ab]])
                P.op("dve", lambda e, ab=ab, Y=Y, hb=hb, v=v: e.scalar_tensor_tensor(
                    out=s_sb[ab][:, 512:640], in0=ps[Y][:, 0:128], scalar=80.0, in1=nb_sb[hb][:, v, 512:640],
                    op0=ALU.min, op1=ALU.add), reads=[r_ps[Y], r_nb[hb]], writes=[r_s[ab]])
                P.op("act", lambda e, ab=ab: e.activation(out=pt_sb[ab][:], in_=s_sb[ab][:], func=AF.Exp),
                     reads=[r_s[ab]], writes=[r_pt[ab]])

                def pv(e, b=b, hb=hb, ab=ab, Y=Y):
                    ins = None
                    for i in range(5):
                        ins = e.matmul(ps[Y][:, 128:256], Vh[hb][:, b + i, :], pt_sb[ab][:, i * 128:(i + 1) * 128],
                                       start=(i == 0), stop=(i == 4))
                    for i in range(5):
                        ins = e.matmul(ps[Y][:, 256:384], ones1[:], pt_sb[ab][:, i * 128:(i + 1) * 128],
                                       start=(i == 0), stop=(i == 4))
                    return ins
                P.op("pe", pv, reads=[r_Vh[hb], r_pt[ab], r_ones], writes=[r_ps[Y]])
                P.op("dve", lambda e, ab=ab, Y=Y: e.reciprocal(out=rc_sb[ab][:], in_=ps[Y][:, 256:384]),
                     reads=[r_ps[Y]], writes=[r_rc[ab]])
                P.op("dve", lambda e, ab=ab, Y=Y, h=h, b=b: e.tensor_tensor(
                    out=B[:, h, b * 128:(b + 1) * 128], in0=ps[Y][:, 128:256], in1=rc_sb[ab][:], op=ALU.mult),
                    reads=[r_ps[Y], r_rc[ab]], writes=[r_B[h]])
        dead = r_A + r_QT + r_KT + r_VT + r_Vh + r_nb + r_s + r_pt + r_rc + [r_stg]
        for g in range(4):
            P.dma("sp", lambda e, g=g: e.dma_start(out=H[:, 4 * g:4 * g + 4, :], in_=hband[:, 4 * g:4 * g + 4, OFF:OFF + TOK]),
                  f"hin{g}", writes=r_H[4 * g:4 * g + 4] + dead)
        for p in range(8):
            s = load_piece(w_o, p, 256)
            for fc in range(2):
                j = 2 * p + fc
                for tt in range(2):
                    bank = next_bank()
                    mm_group(bank, 512, lambda k, s=s, fc=fc: ring[s][:, k, fc * 128:(fc + 1) * 128],
                             lambda k, tt=tt: B[:, k, tt * 512:(tt + 1) * 512], KC, [r_ring[s]] + r_B)
                    P.op("dve", lambda e, bank=bank, j=j, tt=tt: e.scalar_tensor_tensor(
                        out=H[:, j, tt * 512:(tt + 1) * 512], in0=ps[bank][:], scalar=vec(V_BO, j),
                        in1=H[:, j, tt * 512:(tt + 1) * 512], op0=ALU.add, op1=ALU.add),
                        reads=[r_ps[bank], r_H[j], r_vt], writes=[r_H[j]])
        r_A2 = [Res(f"A2_{j}") for j in range(KC)]
        for j in range(KC):
            r_A2[j].readers = []
        mlp(A2, r_A2, B, r_B)
        store_out()

    P.emit()
    return nc


def fm(x):
    t = x.shape[0]
    return np.ascontiguousarray(x.T.reshape(KC, 128, t).transpose(1, 0, 2))


def unfm(y):
    t = y.shape[2]
    return np.ascontiguousarray(y.transpose(1, 0, 2).reshape(D, t).T)


def pieces(w, fw):
    f = w.shape[1]
    return np.ascontiguousarray(w.reshape(KC, 128, f // fw, fw).transpose(2, 1, 0, 3))


def vec16(v):
    return np.ascontiguousarray(v.reshape(KC, 128).T)


def band_rows(c, kind):
    t0 = c * TOK
    if kind == "conv":
        idx = np.arange(t0 - 16, t0 + TOK + 16)
        return np.clip(idx, 0, NCORE * TOK - 1)
    up = np.arange(t0 - 256, t0) if c > 0 else np.arange(256, 512)
    dn = np.arange(t0 + TOK, t0 + TOK + 256) if c < NCORE - 1 else np.arange(t0 + 512, t0 + 768)
    return np.concatenate([up, np.arange(t0, t0 + TOK), dn])


def na_tables(rpb, c):
    srow = np.zeros(24, np.int64)
    for s in range(-4, 20):
        srow[s + 4] = band_rows(c, "na")[(s + 4) * 64] // 64
    kc = np.arange(64)[:, None]
    qc = np.arange(64)[None, :]
    cs = np.clip(qc - 8, 0, 48)
    colok = (kc >= cs) & (kc < cs + 16)
    cidx = np.clip(kc - qc + 15, 0, 30)
    out = np.full((16, 128, 5, 640), NEG, np.float32)

    def fill(var, b):
        for a in range(2):
            r = 16 * c + 2 * b + a
            rs = min(max(r - 4, 0), 120)
            used = set()
            order = sorted(range(10), key=lambda u: (not (0 <= 2 * b - 4 + u <= 15), u))
            for u in order:
                s = 2 * b - 4 + u
                kr = srow[s + 4]
                if kr < rs or kr >= rs + 8 or kr in used:
                    continue
                used.add(kr)
                i, s2 = u // 2, u % 2
                ri = kr - r + 7
                blk = np.where(colok[None], rpb[:, ri][:, cidx], np.float32(NEG))
                out[:, s2 * 64:(s2 + 1) * 64, var, i * 128 + a * 64:i * 128 + (a + 1) * 64] = blk
            assert len(used) == 8, (c, b, a, used)

    fill(0, 3)
    fill(1, 0)
    fill(2, 1)
    fill(3, 6)
    fill(4, 7)
    return out


_NC_CACHE = {}


def _get_nc(kind, last):
    key = (kind, last)
    if key not in _NC_CACHE:
        _NC_CACHE[key] = build_layer(kind, last)
    return _NC_CACHE[key]


def kernel(x, norm_mix_g, norm_ffn_g, final_norm_g,
           conv_w_pw1, conv_b_pw1, conv_w_dw, conv_b_dw, conv_ln_g, conv_ln_b,
           conv_w_pw2, conv_b_pw2,
           na_w_qkv, na_b_qkv, na_rpb, na_w_o, na_b_o,
           ffn_w_up, ffn_w_down):
    f = lambda a: np.asarray(a, np.float32)
    h = f(x)[0]
    ident = np.eye(128, dtype=np.float32)
    for i in range(4):
        j = i // 2
        kind = "conv" if i % 2 == 0 else "na"
        last = i == 3
        nc = _get_nc(kind, last)
        vecs = np.zeros((128, 16, KC), np.float32)
        vecs[:, 0] = vec16(f(norm_mix_g)[i])
        vecs[:, 1] = vec16(f(norm_ffn_g)[i])
        vecs[:, 2] = vec16(f(final_norm_g))
        common = {
            "ident": ident,
            "w_up": pieces(f(ffn_w_up)[i], 256),
            "w_dn": np.ascontiguousarray(
                f(ffn_w_down)[i].reshape(4, KC, 128, 8, 256).transpose(0, 3, 2, 1, 4).reshape(32, 128, KC, 256)),
        }
        in_maps = []
        if kind == "conv":
            w1 = f(conv_w_pw1)[j]
            w1p = np.concatenate([w1[:, :D].reshape(D, KC, 1, 128), w1[:, D:].reshape(D, KC, 1, 128)], axis=2).reshape(D, 2 * D)
            b1 = f(conv_b_pw1)[j]
            vecs[:, 3] = vec16(b1[:D])
            vecs[:, 4] = vec16(b1[D:])
            vecs[:, 5] = vec16(f(conv_b_dw)[j])
            vecs[:, 6] = vec16(f(conv_ln_g)[j])
            vecs[:, 7] = vec16(f(conv_ln_b)[j])
            vecs[:, 8] = vec16(f(conv_b_pw2)[j])
            common["w_pw1"] = pieces(w1p, 256)
            common["w_pw2"] = pieces(f(conv_w_pw2)[j], 256)
            common["wdw"] = np.ascontiguousarray(f(conv_w_dw)[j].T.reshape(KC, 128, CW).transpose(1, 0, 2))
            for c in range(NCORE):
                hm = np.ones((128, 2, KC, 16), np.float32)
                if c == 0:
                    hm[:, 0] = 0.0
                if c == NCORE - 1:
                    hm[:, 1] = 0.0
                m = dict(common)
                m["hband"] = fm(h[band_rows(c, kind)])
                m["vecs"] = vecs
                m["hmask"] = hm
                in_maps.append(m)
        else:
            wq = f(na_w_qkv)[j]
            wqp = np.concatenate([wq[:, 0:D].reshape(D, 16, 1, 128), wq[:, D:2 * D].reshape(D, 16, 1, 128),
                                  wq[:, 2 * D:].reshape(D, 16, 1, 128)], axis=2).reshape(D, 3 * D)
            bq = f(na_b_qkv)[j]
            vecs[:, 3] = vec16(bq[0:D])
            vecs[:, 4] = vec16(bq[D:2 * D])
            vecs[:, 5] = vec16(bq[2 * D:])
            vecs[:, 6] = vec16(f(na_b_o)[j])
            common["w_qkv"] = pieces(wqp, 384)
            common["w_o"] = pieces(f(na_w_o)[j], 256)
            rpb = f(na_rpb)[j]
            tabs = {}
            for c in range(NCORE):
                key = 0 if c == 0 else (2 if c == NCORE - 1 else 1)
                if key not in tabs:
                    tabs[key] = na_tables(rpb, c)
                m = dict(common)
                m["hband"] = fm(h[band_rows(c, kind)])
                m["vecs"] = vecs
                m["nab"] = tabs[key]
                in_maps.append(m)
        res = run_bass_kernel_spmd(nc, in_maps, core_ids=list(range(NCORE)))
        h = np.concatenate([unfm(res.results[c]["hout"]) for c in range(NCORE)], axis=0)
    return h[None].astype(np.float32)
```

```python
import numpy as np
import concourse.bass as bass
import concourse.mybir as mybir
from concourse.bass_utils import run_bass_kernel_spmd

F32 = mybir.dt.float32
BF16 = mybir.dt.bfloat16
AF = mybir.ActivationFunctionType
ALU = mybir.AluOpType

D = 2048
KC = 16
TOK = 1024
NCORE = 8
EPS = 1e-5
NEG = -30000.0
CW = 31
VAR_OF_BLOCK = [1, 2, 0, 0, 0, 0, 3, 4]


class Res:
    __slots__ = ("name", "w", "readers")

    def __init__(self, name):
        self.name = name
        self.w = None
        self.readers = []


class Planner:
    ENGS = ("pe", "act", "dve", "pool", "sp")

    def __init__(self, nc):
        self.nc = nc
        self.ops = {k: [] for k in self.ENGS}
        self.cnt = {k: 0 for k in self.ENGS}
        self.sem = {}
        self.seen = {k: {} for k in self.ENGS}
        self.dma_sems = {}
        self._keep = []
        for k in self.ENGS:
            self.sem[k] = self._new_sem("c_" + k)

    def _new_sem(self, name):
        cm = self.nc.semaphore(name)
        h = cm.__enter__()
        self._keep.append(cm)
        return h

    def _collect(self, eng, reads, writes, is_dma, after=()):
        need = {}

        def add(c, same_ok):
            if c is None:
                return
            ek, sem, val = c
            if ek == eng and same_ok and not is_dma and eng == "pe":
                return
            key = id(sem)
            if key not in need or need[key][1] < val:
                need[key] = (sem, val)

        for r in reads:
            add(r.w, False)
        for w in list(writes) + list(after):
            add(w.w, True)
            for rd in w.readers:
                add(rd, True)
        out = []
        seen = self.seen[eng]
        for key, (sem, val) in need.items():
            if seen.get(key, 0) >= val:
                continue
            seen[key] = val
            out.append((sem, val))
        return out

    def _commit(self, comp, reads, writes):
        for r in reads:
            r.readers.append(comp)
            if len(r.readers) > 24:
                best = {}
                for c in r.readers:
                    k = id(c[1])
                    if k not in best or best[k][2] < c[2]:
                        best[k] = c
                r.readers = list(best.values())
        for w in writes:
            w.w = comp
            w.readers = []

    def op(self, eng, fn, reads=(), writes=(), after=()):
        waits = self._collect(eng, reads, writes, False, after)
        self.cnt[eng] += 1
        comp = (eng, self.sem[eng], self.cnt[eng])
        self.ops[eng].append((waits, fn, (self.sem[eng], 1)))
        self._commit(comp, reads, writes)
        return comp

    def dma(self, eng, fn, slot, reads=(), writes=()):
        waits = self._collect(eng, reads, writes, True)
        if slot not in self.dma_sems:
            self.dma_sems[slot] = [self._new_sem("d_" + slot), 0]
        ent = self.dma_sems[slot]
        ent[1] += 16
        comp = ("dma:" + slot, ent[0], ent[1])
        self.ops[eng].append((waits, fn, (ent[0], 16)))
        self._commit(comp, reads, writes)
        return comp

    def barrier(self):
        targets = [(self.sem[k], self.cnt[k]) for k in self.ENGS if self.cnt[k] > 0]
        targets += [(ent[0], ent[1]) for ent in self.dma_sems.values()]
        for k in self.ENGS:
            waits = []
            for sem, val in targets:
                if sem is self.sem[k]:
                    continue
                if self.seen[k].get(id(sem), 0) >= val:
                    continue
                self.seen[k][id(sem)] = val
                waits.append((sem, val))
            self.ops[k].append((waits, None, None))

    def final_wait(self, eng, slots):
        waits = [(self.dma_sems[s][0], self.dma_sems[s][1]) for s in slots]
        self.ops[eng].append((waits, None, None))

    def emit(self):
        ops = self.ops

        def run(e, lst):
            for waits, fn, inc in lst:
                for sem, val in waits:
                    e.wait_ge(sem, val)
                if fn is not None:
                    fn(e).then_inc(inc[0], inc[1])

        with self.nc.Block() as block:
            @block.tensor
            def _(e):
                run(e, ops["pe"])

            @block.scalar
            def _(e):
                run(e, ops["act"])

            @block.vector
            def _(e):
                run(e, ops["dve"])

            @block.gpsimd
            def _(e):
                run(e, ops["pool"])

            @block.sync
            def _(e):
                run(e, ops["sp"])


XB_T0 = -656
H0_T0 = -640
H1_T0 = -384
H2_T0 = -256
XB_W, H0_W, H1_W, H2_W = 2336, 2304, 1792, 1536
PASSES = [
    ("conv", 0, -640, 896), ("conv", 0, 256, 896), ("conv", 0, 1152, 512),
    ("na", 1, -6, 7, {3: 1, 4: 2}, (4, 8), None),
    ("na", 1, 8, 7, {2: 3, 3: 4}, None, (6, 2)),
    ("conv", 2, -256, 1024), ("conv", 2, 768, 512),
    ("na", 3, 0, 8, {0: 1, 1: 2, 6: 3, 7: 4}, (1, 5), (10, 6)),
]
SRC_T0 = {0: XB_T0, 1: H0_T0, 2: H1_T0, 3: H2_T0}
DST_T0 = {0: H0_T0, 1: H1_T0, 2: H2_T0, 3: 0}


def even_tiles(n):
    k = (n + 511) // 512
    step = (n + k - 1) // k
    return tiles_of(n, step)


def tiles_of(n, step=512):
    out = []
    t = 0
    while t < n:
        out.append((t, min(step, n - t)))
        t += step
    return out


def build_fused():
    nc = bass.Bass("TRN2", target_bir_lowering=False)
    P = Planner(nc)

    def din(name, shape, dt=F32):
        return nc.dram_tensor(name, list(shape), dt, kind="ExternalInput").ap()

    xb = din("xb", [128, KC, XB_W])
    vecs = din("vecs", [4, 128, 16, KC])
    identd = din("ident", [128, 128])
    w_up = [din(f"w_up{i}", [32, 128, KC, 256]) for i in range(4)]
    w_dn = [din(f"w_dn{i}", [32, 128, KC, 256]) for i in range(4)]
    w_pw1 = {i: din(f"w_pw1_{i}", [16, 128, KC, 256]) for i in (0, 2)}
    w_pw2 = {i: din(f"w_pw2_{i}", [8, 128, KC, 256]) for i in (0, 2)}
    wdw = {i: din(f"wdw{i}", [128, KC, CW]) for i in (0, 2)}
    w_qkv = {i: din(f"w_qkv{i}", [16, 128, KC, 384]) for i in (1, 3)}
    w_o = {i: din(f"w_o{i}", [8, 128, KC, 256]) for i in (1, 3)}
    n_conv = sum(1 for p in PASSES if p[0] == "conv")
    n_na = sum(1 for p in PASSES if p[0] == "na")
    cmask = din("cmask", [n_conv, 128, 1056])
    nab = din("nab", [n_na, 16, 128, 5, 640])
    hout = nc.dram_tensor("hout", [128, KC, TOK], F32, kind="ExternalOutput").ap()
    scr = {0: xb,
           1: nc.dram_tensor("h0s", [128, KC, H0_W], F32).ap(),
           2: nc.dram_tensor("h1s", [128, KC, H1_W], F32).ap(),
           3: nc.dram_tensor("h2s", [128, KC, H2_W], F32).ap()}

    V_GMIX, V_GFFN, V_GFIN = 0, 1, 2
    V_BA, V_BG, V_BDW, V_LNG, V_LNB, V_BPW2 = 3, 4, 5, 6, 7, 8
    V_BQ, V_BK, V_BV, V_BO = 3, 4, 5, 6
    V_MTOP, V_NTOP, V_MBOT, V_NBOT = 12, 13, 14, 15

    SB_BASE = 16576
    off = [SB_BASE]

    def alloc(name, shape, dt):
        nbytes = int(np.prod(shape[1:])) * (4 if dt == F32 else 2)
        at = off[0]
        off[0] = at + ((nbytes + 63) // 64) * 64
        return nc.alloc_sbuf_tensor_at(name, list(shape), dt, offset=at), at

    NSLOT = 3
    ring = [alloc(f"ring{i}", [128, KC, 384], BF16)[0] for i in range(NSLOT)]
    r_ring = [Res(f"ring{i}") for i in range(NSLOT)]
    vt, _ = alloc("vecs_sb", [128, 16, KC], F32)
    ident, _ = alloc("ident_sb", [128, 128], BF16)
    onesD, _ = alloc("onesD", [128, 128], BF16)
    ones1, _ = alloc("ones1", [128, 128], BF16)
    epst, _ = alloc("epst", [128, 16], F32)
    sq = [alloc(f"sq{i}", [128, 512], BF16)[0] for i in range(2)]
    r_sq = [Res(f"sq{i}") for i in range(2)]
    rstd_t = [alloc(f"rstd{i}", [128, 512], F32)[0] for i in range(2)]
    r_rstd = [Res(f"rstd{i}") for i in range(2)]
    tmpa = [alloc(f"tmpa{i}", [128, 512], F32)[0] for i in range(2)]
    r_tmpa = [Res(f"tmpa{i}") for i in range(2)]
    base = off[0]
    r_vt, r_ident, r_ones = Res("vt"), Res("ident"), Res("ones")
    r_stg = Res("stg")
    r_H = [Res(f"H{j}") for j in range(KC)]
    r_A = [Res(f"A{j}") for j in range(KC)]
    r_B = [Res(f"B{j}") for j in range(KC)]

    off[0] = base
    cH, _ = alloc("cH", [128, KC, TOK], F32)
    cA, _ = alloc("cA", [128, KC, 1056], BF16)
    cB, _ = alloc("cB", [128, KC, 1056], BF16)
    cstg, _ = alloc("cstg", [128, KC, 32], F32)
    hh, _ = alloc("hh", [128, KC, 32], BF16)
    r_hh = Res("hh")
    wdw_sb, _ = alloc("wdw_sb", [128, KC, CW], F32)
    cm_sb, _ = alloc("cm_sb", [128, 1056], BF16)
    dg = [alloc(f"dg{i}", [128, CW, 128], BF16)[0] for i in range(2)]
    r_dg = [Res(f"dg{i}") for i in range(2)]
    mean_sb = [alloc(f"mean{i}", [128, 512], F32)[0] for i in range(2)]
    r_mean = [Res(f"mean{i}") for i in range(2)]
    _tb, _ = alloc("tmpb", [128, 512], F32)
    tmpb = [_tb, _tb]
    _rtb = Res("tmpb")
    r_tmpb = [_rtb, _rtb]
    r_wdw, r_cm = Res("wdw"), Res("cm")
    assert off[0] <= 229344, off[0]

    off[0] = base
    nA, a_at = alloc("nA", [128, KC, 1536], BF16)
    QT = [alloc(f"QT{i}", [128, TOK], BF16)[0] for i in range(2)]
    KT = [alloc(f"KT{i}", [128, 1536], BF16)[0] for i in range(2)]
    VT = [alloc(f"VT{i}", [128, 1536], BF16)[0] for i in range(2)]
    Vh = [alloc(f"Vh{i}", [128, 12, 128], BF16)[0] for i in range(2)]
    nb_sb = [alloc(f"nab{i}", [128, 5, 640], F32)[0] for i in range(2)]
    s_sb = [alloc(f"s_sb{i}", [128, 640], F32)[0] for i in range(2)]
    pt_sb = [alloc(f"pt_sb{i}", [128, 640], BF16)[0] for i in range(2)]
    rc_sb = [alloc(f"rc_sb{i}", [128, 128], F32)[0] for i in range(2)]
    bl_sb, _ = alloc("bl_sb", [128, 128], BF16)
    psm = [alloc(f"psm{i}", [128, 128], BF16)[0] for i in range(2)]
    r_psm = [Res(f"psm{i}") for i in range(2)]
    nstg2 = [alloc(f"nstg{i}", [128, KC, 128], F32)[0] for i in range(2)]
    r_nstg = [Res(f"nstg{i}") for i in range(2)]
    att_end = off[0]
    nB, _ = alloc("nB", [128, KC, TOK], BF16)
    nH = nc.alloc_sbuf_tensor_at("nH", [128, KC, TOK], F32, offset=a_at)
    nA2 = nc.alloc_sbuf_tensor_at("nA2", [128, KC, TOK], BF16, offset=a_at + KC * TOK * 4)
    assert a_at + KC * TOK * 6 <= att_end, (a_at, att_end)
    assert off[0] <= 229344, off[0]
    r_QT = [Res(f"QT{i}") for i in range(2)]
    r_KT = [Res(f"KT{i}") for i in range(2)]
    r_VT = [Res(f"VT{i}") for i in range(2)]
    r_Vh = [Res(f"Vh{i}") for i in range(2)]
    r_nb = [Res(f"nab{i}") for i in range(2)]
    r_s = [Res(f"s_sb{i}") for i in range(2)]
    r_pt = [Res(f"pt_sb{i}") for i in range(2)]
    r_rc = [Res(f"rc_sb{i}") for i in range(2)]
    r_bl = Res("bl")

    ps = [nc.alloc_psum_tensor(f"ps{i}", [128, 512], F32) for i in range(8)]
    r_ps = [Res(f"ps{i}") for i in range(8)]
    gb = [0]

    def next_bank(nb=3):
        b = gb[0] % nb
        gb[0] += 1
        return b

    P.dma("pool", lambda e: e.dma_start(out=ident[:], in_=identd), "ident", writes=[r_ident])
    P.op("dve", lambda e: e.memset(onesD[:], 1.0 / D), writes=[r_ones])
    P.op("dve", lambda e: e.memset(ones1[:], 1.0), writes=[r_ones])
    P.op("dve", lambda e: e.memset(epst[:], EPS), writes=[r_ones])

    def vec(slot, j):
        return vt[:, slot, j:j + 1]

    def rsqrt_to(i, src_ap, src_res, n):
        P.op("act", lambda e: e.activation(out=rstd_t[i][:, 0:n], in_=src_ap, func=AF.Sqrt, bias=epst[:, 0:1]),
             reads=src_res + [r_ones], writes=[r_rstd[i]])
        P.op("dve", lambda e: e.reciprocal(out=rstd_t[i][:, 0:n], in_=rstd_t[i][:, 0:n]),
             reads=[r_rstd[i]], writes=[r_rstd[i]])

    pc = [0]

    def load_piece(wd, p, fw):
        s = pc[0] % NSLOT
        pc[0] += 1
        P.dma("pool", lambda e: e.dma_start(out=ring[s][:, :, 0:fw], in_=wd[p]), f"ring{s}", writes=[r_ring[s]])
        return s

    def mm_group(bank, n, lhs_fn, rhs_fn, nk, reads):
        def f(e):
            ins = None
            for k in range(nk):
                ins = e.matmul(ps[bank][:, 0:n], lhs_fn(k), rhs_fn(k), start=(k == 0), stop=(k == nk - 1))
            return ins
        P.op("pe", f, reads=reads, writes=[r_ps[bank]])

    nrm = [0]

    def rms_tile(src_fn, src_res, n, gslot, dst_fn, dst_res_fn):
        i = nrm[0] % 2
        nrm[0] += 1
        bank = 3 + (nrm[0] % 2)
        for j in range(KC):
            q = j % 2
            P.op("act", lambda e, j=j, q=q: e.activation(out=sq[q][:, 0:n], in_=src_fn(j), func=AF.Square),
                 reads=src_res(j), writes=[r_sq[q]])
            P.op("pe", lambda e, j=j, q=q: e.matmul(ps[bank][:, 0:n], onesD[:], sq[q][:, 0:n],
                                                    start=(j == 0), stop=(j == KC - 1)),
                 reads=[r_sq[q], r_ones], writes=[r_ps[bank]] if j == 0 else [])
        r_ps[bank].w = ("pe", P.sem["pe"], P.cnt["pe"])
        rsqrt_to(i, ps[bank][:, 0:n], [r_ps[bank]], n)
        for j in range(KC):
            P.op("dve", lambda e, j=j: e.scalar_tensor_tensor(out=dst_fn(j), in0=src_fn(j), scalar=vec(gslot, j),
                                                              in1=rstd_t[i][:, 0:n], op0=ALU.mult, op1=ALU.mult),
                 reads=src_res(j) + [r_rstd[i], r_vt], writes=dst_res_fn(j))

    def mlp(li, H, HN, r_HN, UP, r_UP, tts):
        for (t0, n) in tts:
            rms_tile(lambda j, t0=t0, n=n: H[:, j, t0:t0 + n], lambda j: [r_H[j]], n, V_GFFN,
                     lambda j, t0=t0, n=n: HN[:, j, t0:t0 + n], lambda j: [r_HN[j]])
        for blk in range(4):
            for pp in range(8):
                s = load_piece(w_up[li], blk * 8 + pp, 256)
                for fc in range(2):
                    c = 2 * pp + fc
                    for (t0, n) in tts:
                        bank = next_bank()
                        mm_group(bank, n, lambda k, s=s, fc=fc: ring[s][:, k, fc * 128:(fc + 1) * 128],
                                 lambda k, t0=t0, n=n: HN[:, k, t0:t0 + n], KC, [r_ring[s]] + r_HN)
                        q = gb[0] % 2
                        P.op("act", lambda e, bank=bank, q=q, n=n: e.activation(out=tmpa[q][:, 0:n], in_=ps[bank][:, 0:n],
                                                                                func=AF.Relu),
                             reads=[r_ps[bank]], writes=[r_tmpa[q]])
                        P.op("dve", lambda e, q=q, c=c, t0=t0, n=n: e.tensor_tensor(
                            out=UP[:, c, t0:t0 + n], in0=tmpa[q][:, 0:n], in1=tmpa[q][:, 0:n], op=ALU.mult),
                            reads=[r_tmpa[q]], writes=[r_UP[c]])
            for pp in range(8):
                s = load_piece(w_dn[li], blk * 8 + pp, 256)
                for fc in range(2):
                    j = 2 * pp + fc
                    for (t0, n) in tts:
                        bank = next_bank()
                        mm_group(bank, n, lambda k, s=s, fc=fc: ring[s][:, k, fc * 128:(fc + 1) * 128],
                                 lambda k, t0=t0, n=n: UP[:, k, t0:t0 + n], KC, [r_ring[s]] + r_UP)
                        P.op("dve", lambda e, bank=bank, j=j, t0=t0, n=n: e.tensor_tensor(
                            out=H[:, j, t0:t0 + n], in0=ps[bank][:, 0:n], in1=H[:, j, t0:t0 + n], op=ALU.add),
                            reads=[r_ps[bank], r_H[j]], writes=[r_H[j]])

    def store_out(H, dst, d0, n, tts, final, tag):
        if final:
            for (t0, tn) in tts:
                i = nrm[0] % 2
                nrm[0] += 1
                bank = 3 + (nrm[0] % 2)
                for j in range(KC):
                    q = j % 2
                    P.op("act", lambda e, j=j, q=q, t0=t0, tn=tn: e.activation(out=sq[q][:, 0:tn], in_=H[:, j, t0:t0 + tn],
                                                                               func=AF.Square),
                         reads=[r_H[j]], writes=[r_sq[q]])
                    P.op("pe", lambda e, j=j, q=q, bank=bank, tn=tn: e.matmul(ps[bank][:, 0:tn], onesD[:], sq[q][:, 0:tn],
                                                                              start=(j == 0), stop=(j == KC - 1)),
                         reads=[r_sq[q], r_ones], writes=[r_ps[bank]] if j == 0 else [])
                r_ps[bank].w = ("pe", P.sem["pe"], P.cnt["pe"])
                rsqrt_to(i, ps[bank][:, 0:tn], [r_ps[bank]], tn)
                for j in range(KC):
                    P.op("dve", lambda e, j=j, i=i, t0=t0, tn=tn: e.scalar_tensor_tensor(
                        out=H[:, j, t0:t0 + tn], in0=H[:, j, t0:t0 + tn], scalar=vec(V_GFIN, j),
                        in1=rstd_t[i][:, 0:tn], op0=ALU.mult, op1=ALU.mult),
                        reads=[r_H[j], r_rstd[i], r_vt], writes=[r_H[j]])
        slots = []
        for g in range(4):
            P.dma("sp", lambda e, g=g: e.dma_start(out=dst[:, 4 * g:4 * g + 4, d0:d0 + n], in_=H[:, 4 * g:4 * g + 4, 0:n]),
                  f"out{g}", reads=r_H[4 * g:4 * g + 4])
            slots.append(f"out{g}")
        return slots

    def conv_pass(li, a0, n, ci):
        H, A, B = cH, cA, cB
        src = scr[li]
        dst = scr[li + 1]
        s0 = a0 - 16 - SRC_T0[li]
        d0 = a0 - DST_T0[li]
        NB = n + 32
        tts = tiles_of(n)
        P.dma("sp", lambda e: e.dma_start(out=vt[:], in_=vecs[li]), "vt", writes=[r_vt])
        for g in range(4):
            P.dma("sp", lambda e, g=g: e.dma_start(out=H[:, 4 * g:4 * g + 4, 0:n], in_=src[:, 4 * g:4 * g + 4, s0 + 16:s0 + 16 + n]),
                  f"hin{g}", writes=r_H[4 * g:4 * g + 4])
        P.dma("sp", lambda e: e.dma_start(out=cstg[:, :, 0:16], in_=src[:, :, s0:s0 + 16]), "stg", writes=[r_stg])
        P.dma("sp", lambda e: e.dma_start(out=cstg[:, :, 16:32], in_=src[:, :, s0 + 16 + n:s0 + 32 + n]), "stg2", writes=[r_stg])
        P.dma("sp", lambda e: e.dma_start(out=wdw_sb[:], in_=wdw[li]), "wdw", writes=[r_wdw])
        P.dma("pool", lambda e: e.dma_start(out=cm_sb[:], in_=cmask[ci]), "cm", writes=[r_cm])
        for (t0, tn) in tts:
            rms_tile(lambda j, t0=t0, tn=tn: H[:, j, t0:t0 + tn], lambda j: [r_H[j]], tn, V_GMIX,
                     lambda j, t0=t0, tn=tn: A[:, j, 16 + t0:16 + t0 + tn], lambda j: [r_A[j]])
        rms_tile(lambda j: cstg[:, j, 0:32], lambda j: [r_stg], 32, V_GMIX, lambda j: hh[:, j, :], lambda j: [r_hh])
        for j in range(KC):
            P.op("act", lambda e, j=j: e.activation(out=A[:, j, 0:16], in_=hh[:, j, 0:16], func=AF.Identity),
                 reads=[r_hh], writes=[r_A[j]])
            P.op("act", lambda e, j=j: e.activation(out=A[:, j, 16 + n:NB], in_=hh[:, j, 16:32], func=AF.Identity),
                 reads=[r_hh], writes=[r_A[j]])
        for p in range(16):
            s = load_piece(w_pw1[li], p, 256)
            for (t0, tn) in even_tiles(NB):
                banks = []
                for fc in range(2):
                    bank = next_bank()
                    banks.append(bank)
                    mm_group(bank, tn, lambda k, s=s, fc=fc: ring[s][:, k, fc * 128:(fc + 1) * 128],
                             lambda k, t0=t0, tn=tn: A[:, k, t0:t0 + tn], KC, [r_ring[s]] + r_A)
                q = gb[0] % 2
                ba, bg = banks
                P.op("act", lambda e, bg=bg, q=q, tn=tn, p=p: e.activation(out=tmpa[q][:, 0:tn], in_=ps[bg][:, 0:tn],
                                                                           func=AF.Sigmoid, bias=vec(V_BG, p)),
                     reads=[r_ps[bg], r_vt], writes=[r_tmpa[q]])
                P.op("dve", lambda e, ba=ba, q=q, tn=tn, p=p, t0=t0: e.scalar_tensor_tensor(
                    out=B[:, p, t0:t0 + tn], in0=ps[ba][:, 0:tn], scalar=vec(V_BA, p), in1=tmpa[q][:, 0:tn],
                    op0=ALU.add, op1=ALU.mult), reads=[r_ps[ba], r_tmpa[q], r_vt], writes=[r_B[p]])
        for j in range(KC):
            P.op("dve", lambda e, j=j: e.tensor_tensor(out=B[:, j, 0:NB], in0=B[:, j, 0:NB], in1=cm_sb[:, 0:NB], op=ALU.mult),
                 reads=[r_B[j], r_cm], writes=[r_B[j]])
        r_Y = [[Res(f"Y{ti}_{j}") for j in range(KC)] for ti in range(len(tts))]
        r_Z = [[Res(f"Z{ti}_{j}") for j in range(KC)] for ti in range(len(tts))]
        for ti, (t0, tn) in enumerate(tts):
            for j in range(KC):
                di = (ti * KC + j) % 2
                for k in range(CW):
                    P.op("dve", lambda e, j=j, k=k, di=di: e.tensor_scalar(out=dg[di][:, k, :], in0=ident[:],
                                                                           scalar1=wdw_sb[:, j, k:k + 1], scalar2=None, op0=ALU.mult),
                         reads=[r_ident, r_wdw], writes=[r_dg[di]])
                bank = next_bank()
                mm_group(bank, tn, lambda k, di=di: dg[di][:, k, :],
                         lambda k, j=j, t0=t0, tn=tn: B[:, j, t0 + k + 1:t0 + k + 1 + tn], CW, [r_dg[di], r_B[j]])
                P.op("act", lambda e, bank=bank, j=j, t0=t0, tn=tn: e.activation(out=A[:, j, t0:t0 + tn], in_=ps[bank][:, 0:tn],
                                                                                 func=AF.Identity, bias=vec(V_BDW, j)),
                     reads=[r_ps[bank], r_vt], writes=[r_Y[ti][j]], after=[r_A[j]])
            i = ti % 2
            bm, bq = 4 + 2 * i, 5 + 2 * i
            for j in range(KC):
                q = j % 2
                P.op("act", lambda e, j=j, q=q, t0=t0, tn=tn: e.activation(out=sq[q][:, 0:tn], in_=A[:, j, t0:t0 + tn], func=AF.Square),
                     reads=[r_Y[ti][j]], writes=[r_sq[q]])
                P.op("pe", lambda e, j=j, t0=t0, tn=tn, bm=bm: e.matmul(ps[bm][:, 0:tn], onesD[:], A[:, j, t0:t0 + tn],
                                                                        start=(j == 0), stop=(j == KC - 1)),
                     reads=[r_Y[ti][j], r_ones], writes=[r_ps[bm]] if j == 0 else [])
                P.op("pe", lambda e, j=j, q=q, bq=bq, tn=tn: e.matmul(ps[bq][:, 0:tn], onesD[:], sq[q][:, 0:tn],
                                                                      start=(j == 0), stop=(j == KC - 1)),
                     reads=[r_sq[q], r_ones], writes=[r_ps[bq]] if j == 0 else [])
            r_ps[bm].w = ("pe", P.sem["pe"], P.cnt["pe"])
            r_ps[bq].w = ("pe", P.sem["pe"], P.cnt["pe"])
            P.op("act", lambda e, i=i, bm=bm, tn=tn: e.activation(out=mean_sb[i][:, 0:tn], in_=ps[bm][:, 0:tn], func=AF.Identity),
                 reads=[r_ps[bm]], writes=[r_mean[i]])
            P.op("dve", lambda e, i=i, tn=tn: e.tensor_tensor(out=tmpb[i][:, 0:tn], in0=mean_sb[i][:, 0:tn], in1=mean_sb[i][:, 0:tn],
                                                              op=ALU.mult), reads=[r_mean[i]], writes=[r_tmpb[i]])
            P.op("dve", lambda e, i=i, bq=bq, tn=tn: e.tensor_tensor(out=tmpb[i][:, 0:tn], in0=ps[bq][:, 0:tn], in1=tmpb[i][:, 0:tn],
                                                                     op=ALU.subtract),
                 reads=[r_ps[bq], r_tmpb[i]], writes=[r_tmpb[i]])
            rsqrt_to(i, tmpb[i][:, 0:tn], [r_tmpb[i]], tn)
            for j in range(KC):
                q = j % 2
                P.op("dve", lambda e, j=j, q=q, i=i, t0=t0, tn=tn: e.tensor_tensor(out=tmpa[q][:, 0:tn], in0=A[:, j, t0:t0 + tn],
                                                                                   in1=mean_sb[i][:, 0:tn], op=ALU.subtract),
                     reads=[r_Y[ti][j], r_mean[i]], writes=[r_tmpa[q]])
                P.op("dve", lambda e, j=j, q=q, i=i, tn=tn: e.scalar_tensor_tensor(out=tmpa[q][:, 0:tn], in0=tmpa[q][:, 0:tn],
                                                                                   scalar=vec(V_LNG, j), in1=rstd_t[i][:, 0:tn],
                                                                                   op0=ALU.mult, op1=ALU.mult),
                     reads=[r_tmpa[q], r_rstd[i], r_vt], writes=[r_tmpa[q]])
                P.op("act", lambda e, j=j, q=q, t0=t0, tn=tn: e.activation(out=B[:, j, t0:t0 + tn], in_=tmpa[q][:, 0:tn],
                                                                           func=AF.Silu, bias=vec(V_LNB, j)),
                     reads=[r_tmpa[q], r_vt], writes=[r_Z[ti][j]], after=[r_B[j]])
        for ti, (t0, tn) in enumerate(tts):
            for p in range(8):
                s = load_piece(w_pw2[li], p, 256)
                for fc in range(2):
                    j = 2 * p + fc
                    bank = next_bank()
                    mm_group(bank, tn, lambda k, s=s, fc=fc: ring[s][:, k, fc * 128:(fc + 1) * 128],
                             lambda k, t0=t0, tn=tn: B[:, k, t0:t0 + tn], KC, [r_ring[s]] + r_Z[ti] + r_B)
                    P.op("dve", lambda e, bank=bank, j=j, t0=t0, tn=tn: e.scalar_tensor_tensor(
                        out=H[:, j, t0:t0 + tn], in0=ps[bank][:, 0:tn], scalar=vec(V_BPW2, j),
                        in1=H[:, j, t0:t0 + tn], op0=ALU.add, op1=ALU.add),
                        reads=[r_ps[bank], r_H[j], r_vt], writes=[r_H[j]])
        for ti in range(len(tts)):
            for j in range(KC):
                r_A[j].readers.extend(r_Y[ti][j].readers)
        mlp(li, H, A, r_A, B, r_B, tts)
        return store_out(H, dst, d0, n, tts, False, f"c{ci}")

    def na_pass(li, first_row, nblk, varmap, topfix, botfix, ni):
        A, B, H, A2 = nA, nB, nH, nA2
        src = scr[li]
        final = li == 3
        dst = hout if final else scr[li + 1]
        n = nblk * 128
        ntile = nblk + 4
        NB = ntile * 128
        s0 = first_row * 64 - 256 - SRC_T0[li]
        d0 = first_row * 64 - DST_T0[li]
        tts = tiles_of(n)
        P.dma("sp", lambda e: e.dma_start(out=vt[:], in_=vecs[li]), "vt", writes=[r_vt])
        for t in range(ntile):
            sb = t % 2
            P.dma("sp", lambda e, t=t, sb=sb: e.dma_start(out=nstg2[sb][:], in_=src[:, :, s0 + t * 128:s0 + (t + 1) * 128]),
                  f"nstg{sb}", writes=[r_nstg[sb]])
            rms_tile(lambda j, sb=sb: nstg2[sb][:, j, :], lambda j, sb=sb: [r_nstg[sb]], 128, V_GMIX,
                     lambda j, t=t: A[:, j, t * 128:(t + 1) * 128], lambda j: [r_A[j]])
        pT = ps[3][:].bitcast(BF16)

        def gemm_steps(h):
            hb = h % 2
            hold = {}
            steps = []

            def first():
                P.dma("sp", lambda e: e.dma_start(out=nb_sb[hb][:], in_=nab[ni, h]), f"nab{hb}", writes=[r_nb[hb]])
                hold["s"] = load_piece(w_qkv[li], h, 384)
            steps.append(first)

            def qstep(t0, tn):
                s = hold["s"]
                bank = next_bank()
                mm_group(bank, tn, lambda k: ring[s][:, k, 0:128],
                         lambda k: A[:, k, 256 + t0:256 + t0 + tn], KC, [r_ring[s]] + r_A)
                P.op("dve", lambda e: e.tensor_scalar(
                    out=QT[hb][:, t0:t0 + tn], in0=ps[bank][:, 0:tn], scalar1=vec(V_BQ, h), scalar2=128 ** -0.5,
                    op0=ALU.add, op1=ALU.mult), reads=[r_ps[bank], r_vt], writes=[r_QT[hb]])

            def kvstep(fc, dstb, r_dst, vs, t0, tn):
                s = hold["s"]
                bank = next_bank()
                mm_group(bank, tn, lambda k: ring[s][:, k, fc * 128:(fc + 1) * 128],
                         lambda k: A[:, k, t0:t0 + tn], KC, [r_ring[s]] + r_A)
                P.op("act", lambda e: e.activation(
                    out=dstb[hb][:, t0:t0 + tn], in_=ps[bank][:, 0:tn], func=AF.Identity, bias=vec(vs, h)),
                    reads=[r_ps[bank], r_vt], writes=[r_dst[hb]])

            def trstep(g0, cnt):
                def tr(e):
                    ins = None
                    for i in range(cnt):
                        kt = g0 + i
                        ins = e.transpose(pT[:, i * 128:(i + 1) * 128], VT[hb][:, kt * 128:(kt + 1) * 128], ident[:])
                    return ins
                P.op("pe", tr, reads=[r_VT[hb], r_ident], writes=[r_ps[3]])
                P.op("act", lambda e: e.activation(
                    out=Vh[hb][:, g0:g0 + cnt, :].rearrange("p a b -> p (a b)"), in_=pT[:, 0:cnt * 128], func=AF.Identity),
                    reads=[r_ps[3]], writes=[r_Vh[hb]])

            def fixstep():
                for fix, vm, vn in ((topfix, V_MTOP, V_NTOP), (botfix, V_MBOT, V_NBOT)):
                    if fix is None:
                        continue
                    dt_, st_ = fix
                    for (rbuf, d_ap, s_ap) in (
                            (r_KT, KT[hb][:, dt_ * 128:(dt_ + 1) * 128], KT[hb][:, st_ * 128:(st_ + 1) * 128]),
                            (r_Vh, Vh[hb][:, dt_, :], Vh[hb][:, st_, :])):
                        P.op("dve", lambda e, s_ap=s_ap, vm=vm: e.tensor_scalar(
                            out=bl_sb[:], in0=s_ap, scalar1=vec(vm, 0), scalar2=None, op0=ALU.mult),
                            reads=[rbuf[hb], r_vt], writes=[r_bl])
                        P.op("dve", lambda e, d_ap=d_ap, vn=vn: e.scalar_tensor_tensor(
                            out=d_ap, in0=d_ap, scalar=vec(vn, 0), in1=bl_sb[:], op0=ALU.mult, op1=ALU.add),
                            reads=[rbuf[hb], r_bl, r_vt], writes=[rbuf[hb]])

            for (t0, tn) in tts:
                steps.append(lambda t0=t0, tn=tn: qstep(t0, tn))
            for (fc, dstb, r_dst, vs) in ((1, KT, r_KT, V_BK), (2, VT, r_VT, V_BV)):
                for (t0, tn) in tiles_of(NB):
                    steps.append(lambda fc=fc, dstb=dstb, r_dst=r_dst, vs=vs, t0=t0, tn=tn: kvstep(fc, dstb, r_dst, vs, t0, tn))
            for g0 in range(0, ntile, 4):
                steps.append(lambda g0=g0: trstep(g0, min(4, ntile - g0)))
            steps.append(fixstep)
            return steps

        def qk_stage(idx):
            h, b = divmod(idx, nblk)
            hb = h % 2
            v = varmap.get(b, 0)
            ab = idx % 2
            X, Y = 4 + 2 * ab, 5 + 2 * ab

            def qk(e):
                ins = None
                for i in range(5):
                    o = ps[X][:, i * 128:(i + 1) * 128] if i < 4 else ps[Y][:, 0:128]
                    ins = e.matmul(o, KT[hb][:, (b + i) * 128:(b + i + 1) * 128], QT[hb][:, b * 128:(b + 1) * 128],
                                   start=True, stop=True)
                return ins
            P.op("pe", qk, reads=[r_KT[hb], r_QT[hb]], writes=[r_ps[X], r_ps[Y]])
            P.op("dve", lambda e: e.scalar_tensor_tensor(
                out=s_sb[ab][:, 0:512], in0=ps[X][:], scalar=80.0, in1=nb_sb[hb][:, v, 0:512], op0=ALU.min, op1=ALU.add),
                reads=[r_ps[X], r_nb[hb]], writes=[r_s[ab]])
            P.op("dve", lambda e: e.scalar_tensor_tensor(
                out=s_sb[ab][:, 512:640], in0=ps[Y][:, 0:128], scalar=80.0, in1=nb_sb[hb][:, v, 512:640],
                op0=ALU.min, op1=ALU.add), reads=[r_ps[Y], r_nb[hb]], writes=[r_s[ab]])
            P.op("act", lambda e: e.activation(out=pt_sb[ab][:], in_=s_sb[ab][:], func=AF.Exp),
                 reads=[r_s[ab]], writes=[r_pt[ab]])

        def pv_stage(idx):
            h, b = divmod(idx, nblk)
            hb = h % 2
            ab = idx % 2
            Y = 5 + 2 * ab

            def red(e):
                with nc.allow_low_precision(reason="5-term partial softmax row sums in bf16; the 128-way sum is fp32 in PSUM"):
                    return e.tensor_reduce(out=psm[ab][:], in_=pt_sb[ab][:].rearrange("p (i q) -> p q i", i=5),
                                           axis=mybir.AxisListType.X, op=ALU.add)
            P.op("dve", red, reads=[r_pt[ab]], writes=[r_psm[ab]])

            def pv(e):
                ins = None
                for i in range(5):
                    ins = e.matmul(ps[Y][:, 128:256], Vh[hb][:, b + i, :], pt_sb[ab][:, i * 128:(i + 1) * 128],
                                   start=(i == 0), stop=(i == 4))
                ins = e.matmul(ps[Y][:, 256:384], ones1[:], psm[ab][:], start=True, stop=True)
                return ins
            P.op("pe", pv, reads=[r_Vh[hb], r_pt[ab], r_psm[ab], r_ones], writes=[r_ps[Y]])
            P.op("dve", lambda e: e.reciprocal(out=rc_sb[ab][:], in_=ps[Y][:, 256:384]),
                 reads=[r_ps[Y]], writes=[r_rc[ab]])
            P.op("dve", lambda e: e.tensor_tensor(
                out=B[:, h, b * 128:(b + 1) * 128], in0=ps[Y][:, 128:256], in1=rc_sb[ab][:], op=ALU.mult),
                reads=[r_ps[Y], r_rc[ab]], writes=[r_B[h]])

        for st in gemm_steps(0):
            st()
        pending = []
        nidx = 16 * nblk
        qk_stage(0)
        for idx in range(nidx):
            h, b = divmod(idx, nblk)
            if b == 0 and h + 1 < 16:
                pending = gemm_steps(h + 1)
            if b < nblk - 1:
                if idx + 1 < nidx:
                    qk_stage(idx + 1)
                left = nblk - 1 - b
                take = (len(pending) + left - 1) // left
                for st in pending[:take]:
                    st()
                pending = pending[take:]
            else:
                for st in pending:
                    st()
                pending = []
                if idx + 1 < nidx:
                    qk_stage(idx + 1)
            pv_stage(idx)
        dead = r_A + r_QT + r_KT + r_VT + r_Vh + r_nb + r_s + r_pt + r_rc + r_nstg + r_psm + [r_bl]
        for g in range(4):
            P.dma("sp", lambda e, g=g: e.dma_start(out=H[:, 4 * g:4 * g + 4, 0:n], in_=src[:, 4 * g:4 * g + 4, s0 + 256:s0 + 256 + n]),
                  f"hin{g}", writes=r_H[4 * g:4 * g + 4] + dead)
        for p in range(8):
            s = load_piece(w_o[li], p, 256)
            for fc in range(2):
                j = 2 * p + fc
                for (t0, tn) in tts:
                    bank = next_bank()
                    mm_group(bank, tn, lambda k, s=s, fc=fc: ring[s][:, k, fc * 128:(fc + 1) * 128],
                             lambda k, t0=t0, tn=tn: B[:, k, t0:t0 + tn], KC, [r_ring[s]] + r_B)
                    P.op("dve", lambda e, bank=bank, j=j, t0=t0, tn=tn: e.scalar_tensor_tensor(
                        out=H[:, j, t0:t0 + tn], in0=ps[bank][:, 0:tn], scalar=vec(V_BO, j),
                        in1=H[:, j, t0:t0 + tn], op0=ALU.add, op1=ALU.add),
                        reads=[r_ps[bank], r_H[j], r_vt], writes=[r_H[j]])
        r_A2 = [Res(f"A2_{ni}_{j}") for j in range(KC)]
        mlp(li, H, A2, r_A2, B, r_B, tts)
        return store_out(H, dst, d0, n, tts, final, f"n{ni}")

    ci = ni = 0
    slots = []
    for ps_ in PASSES:
        if ps_[0] == "conv":
            slots = conv_pass(ps_[1], ps_[2], ps_[3], ci)
            ci += 1
        else:
            slots = na_pass(ps_[1], ps_[2], ps_[3], ps_[4], ps_[5], ps_[6], ni)
            ni += 1
        P.barrier()
    P.final_wait("sp", slots)
    P.emit()
    return nc


def fm(x):
    t = x.shape[0]
    return np.ascontiguousarray(x.T.reshape(KC, 128, t).transpose(1, 0, 2))


def unfm(y):
    t = y.shape[2]
    return np.ascontiguousarray(y.transpose(1, 0, 2).reshape(D, t).T)


def pieces(w, fw):
    f = w.shape[1]
    return np.ascontiguousarray(w.reshape(KC, 128, f // fw, fw).transpose(2, 1, 0, 3))


def vec16(v):
    return np.ascontiguousarray(v.reshape(KC, 128).T)


def na_tables(rpb, c, first_row, nblk, varmap, topfix, botfix):
    nrow = 2 * (nblk + 4)
    srow = np.full(nrow, -1, np.int64)
    for s in range(nrow):
        g = 16 * c + first_row - 4 + s
        if 0 <= g < 128:
            srow[s] = g
    if topfix is not None and c == 0:
        srow[2 * topfix[0]:2 * topfix[0] + 2] = srow[2 * topfix[1]:2 * topfix[1] + 2]
    if botfix is not None and c == NCORE - 1:
        srow[2 * botfix[0]:2 * botfix[0] + 2] = srow[2 * botfix[1]:2 * botfix[1] + 2]
    kc = np.arange(64)[:, None]
    qc = np.arange(64)[None, :]
    cs = np.clip(qc - 8, 0, 48)
    colok = (kc >= cs) & (kc < cs + 16)
    cidx = np.clip(kc - qc + 15, 0, 30)
    out = np.full((16, 128, 5, 640), NEG, np.float32)

    def put(var, u, a, ri):
        i, s2 = u // 2, u % 2
        blk = np.where(colok[None], rpb[:, ri][:, cidx], np.float32(NEG))
        out[:, s2 * 64:(s2 + 1) * 64, var, i * 128 + a * 64:i * 128 + (a + 1) * 64] = blk

    for a in range(2):
        for u in range(10):
            dlt = u - 4 - a
            if -4 <= dlt <= 3:
                put(0, u, a, dlt + 7)
    for b, var in varmap.items():
        for a in range(2):
            r = 16 * c + first_row + 2 * b + a
            if not (0 <= r < 128) or (c not in (0, NCORE - 1)):
                for u in range(10):
                    dlt = u - 4 - a
                    if -4 <= dlt <= 3:
                        put(var, u, a, dlt + 7)
                continue
            rs = min(max(r - 4, 0), 120)
            used = set()
            natural = lambda u: srow[2 * b + u] == 16 * c + first_row - 4 + 2 * b + u
            order = sorted(range(10), key=lambda u: (not natural(u), u))
            for u in order:
                kr = srow[2 * b + u]
                if kr < rs or kr >= rs + 8 or kr in used:
                    continue
                used.add(kr)
                put(var, u, a, kr - r + 7)
            assert len(used) == 8, (c, first_row, b, a, sorted(used))
    return out


_NC = []


def kernel(x, norm_mix_g, norm_ffn_g, final_norm_g,
           conv_w_pw1, conv_b_pw1, conv_w_dw, conv_b_dw, conv_ln_g, conv_ln_b,
           conv_w_pw2, conv_b_pw2,
           na_w_qkv, na_b_qkv, na_rpb, na_w_o, na_b_o,
           ffn_w_up, ffn_w_down):
    f = lambda a: np.asarray(a, np.float32)
    xs = f(x)[0]
    if not _NC:
        _NC.append(build_fused())
    nc = _NC[0]
    T = NCORE * TOK
    common = {"ident": np.eye(128, dtype=np.float32)}
    vecs = np.zeros((4, 128, 16, KC), np.float32)
    for i in range(4):
        j = i // 2
        vecs[i, :, 0] = vec16(f(norm_mix_g)[i])
        vecs[i, :, 1] = vec16(f(norm_ffn_g)[i])
        vecs[i, :, 2] = vec16(f(final_norm_g))
        common[f"w_up{i}"] = pieces(f(ffn_w_up)[i], 256)
        common[f"w_dn{i}"] = np.ascontiguousarray(
            f(ffn_w_down)[i].reshape(4, KC, 128, 8, 256).transpose(0, 3, 2, 1, 4).reshape(32, 128, KC, 256))
        if i % 2 == 0:
            w1 = f(conv_w_pw1)[j]
            w1p = np.concatenate([w1[:, :D].reshape(D, KC, 1, 128), w1[:, D:].reshape(D, KC, 1, 128)], axis=2).reshape(D, 2 * D)
            b1 = f(conv_b_pw1)[j]
            vecs[i, :, 3] = vec16(b1[:D])
            vecs[i, :, 4] = vec16(b1[D:])
            vecs[i, :, 5] = vec16(f(conv_b_dw)[j])
            vecs[i, :, 6] = vec16(f(conv_ln_g)[j])
            vecs[i, :, 7] = vec16(f(conv_ln_b)[j])
            vecs[i, :, 8] = vec16(f(conv_b_pw2)[j])
            common[f"w_pw1_{i}"] = pieces(w1p, 256)
            common[f"w_pw2_{i}"] = pieces(f(conv_w_pw2)[j], 256)
            common[f"wdw{i}"] = np.ascontiguousarray(f(conv_w_dw)[j].T.reshape(KC, 128, CW).transpose(1, 0, 2))
        else:
            wq = f(na_w_qkv)[j]
            wqp = np.concatenate([wq[:, 0:D].reshape(D, 16, 1, 128), wq[:, D:2 * D].reshape(D, 16, 1, 128),
                                  wq[:, 2 * D:].reshape(D, 16, 1, 128)], axis=2).reshape(D, 3 * D)
            bq = f(na_b_qkv)[j]
            vecs[i, :, 3] = vec16(bq[0:D])
            vecs[i, :, 4] = vec16(bq[D:2 * D])
            vecs[i, :, 5] = vec16(bq[2 * D:])
            vecs[i, :, 6] = vec16(f(na_b_o)[j])
            common[f"w_qkv{i}"] = pieces(wqp, 384)
            common[f"w_o{i}"] = pieces(f(na_w_o)[j], 256)
    conv_passes = [p for p in PASSES if p[0] == "conv"]
    na_passes = [p for p in PASSES if p[0] == "na"]
    tab_cache = {}
    in_maps = []
    for c in range(NCORE):
        m = dict(common)
        g = np.arange(XB_T0, XB_T0 + XB_W) + c * TOK
        ok = (g >= 0) & (g < T)
        xbnd = np.zeros((XB_W, D), np.float32)
        xbnd[ok] = xs[g[ok]]
        m["xb"] = fm(xbnd)
        vc = vecs.copy()
        vc[:, :, 12] = 1.0 if c == 0 else 0.0
        vc[:, :, 13] = 0.0 if c == 0 else 1.0
        vc[:, :, 14] = 1.0 if c == NCORE - 1 else 0.0
        vc[:, :, 15] = 0.0 if c == NCORE - 1 else 1.0
        m["vecs"] = vc
        cm = np.zeros((len(conv_passes), 128, 1056), np.float32)
        for pi, (_, li, a0, n) in enumerate(conv_passes):
            gg = c * TOK + a0 - 16 + np.arange(n + 32)
            cm[pi, :, :n + 32] = ((gg >= 0) & (gg < T)).astype(np.float32)[None, :]
        m["cmask"] = cm
        key = 0 if c == 0 else (2 if c == NCORE - 1 else 1)
        if key not in tab_cache:
            tab_cache[key] = np.stack([na_tables(f(na_rpb)[p[1] // 2], c, p[2], p[3], p[4], p[5], p[6]) for p in na_passes])
        m["nab"] = tab_cache[key]
        in_maps.append(m)
    res = run_bass_kernel_spmd(nc, in_maps, core_ids=list(range(NCORE)))
    out = np.concatenate([unfm(res.results[c]["hout"]) for c in range(NCORE)], axis=0)
    return out[None].astype(np.float32)
```

```python
import numpy as np
import concourse.bass as bass
import concourse.mybir as mybir
from concourse.bass_utils import run_bass_kernel_spmd

F32 = mybir.dt.float32
BF16 = mybir.dt.bfloat16
AF = mybir.ActivationFunctionType
ALU = mybir.AluOpType

D = 2048
KC = 16
TOK = 1024
NCORE = 8
EPS = 1e-5
NEG = -30000.0
CW = 31
VAR_OF_BLOCK = [1, 2, 0, 0, 0, 0, 3, 4]


class Res:
    __slots__ = ("name", "w", "readers")

    def __init__(self, name):
        self.name = name
        self.w = None
        self.readers = []


class Planner:
    ENGS = ("pe", "act", "dve", "pool", "sp")

    def __init__(self, nc):
        self.nc = nc
        self.ops = {k: [] for k in self.ENGS}
        self.cnt = {k: 0 for k in self.ENGS}
        self.sem = {}
        self.seen = {k: {} for k in self.ENGS}
        self.dma_sems = {}
        self._keep = []
        for k in self.ENGS:
            self.sem[k] = self._new_sem("c_" + k)

    def _new_sem(self, name):
        cm = self.nc.semaphore(name)
        h = cm.__enter__()
        self._keep.append(cm)
        return h

    def _collect(self, eng, reads, writes, is_dma, after=()):
        need = {}

        def add(c, same_ok):
            if c is None:
                return
            ek, sem, val = c
            if ek == eng and same_ok and not is_dma and eng == "pe":
                return
            key = id(sem)
            if key not in need or need[key][1] < val:
                need[key] = (sem, val)

        for r in reads:
            add(r.w, False)
        for w in list(writes) + list(after):
            add(w.w, True)
            for rd in w.readers:
                add(rd, True)
        out = []
        seen = self.seen[eng]
        for key, (sem, val) in need.items():
            if seen.get(key, 0) >= val:
                continue
            seen[key] = val
            out.append((sem, val))
        return out

    def _commit(self, comp, reads, writes):
        for r in reads:
            r.readers.append(comp)
            if len(r.readers) > 24:
                best = {}
                for c in r.readers:
                    k = id(c[1])
                    if k not in best or best[k][2] < c[2]:
                        best[k] = c
                r.readers = list(best.values())
        for w in writes:
            w.w = comp
            w.readers = []

    def op(self, eng, fn, reads=(), writes=(), after=()):
        waits = self._collect(eng, reads, writes, False, after)
        self.cnt[eng] += 1
        comp = (eng, self.sem[eng], self.cnt[eng])
        self.ops[eng].append((waits, fn, (self.sem[eng], 1)))
        self._commit(comp, reads, writes)
        return comp

    def dma(self, eng, fn, slot, reads=(), writes=()):
        waits = self._collect(eng, reads, writes, True)
        if slot not in self.dma_sems:
            self.dma_sems[slot] = [self._new_sem("d_" + slot), 0]
        ent = self.dma_sems[slot]
        ent[1] += 16
        comp = ("dma:" + slot, ent[0], ent[1])
        self.ops[eng].append((waits, fn, (ent[0], 16)))
        self._commit(comp, reads, writes)
        return comp

    def barrier(self):
        targets = [(self.sem[k], self.cnt[k]) for k in self.ENGS if self.cnt[k] > 0]
        targets += [(ent[0], ent[1]) for ent in self.dma_sems.values()]
        for k in self.ENGS:
            waits = []
            for sem, val in targets:
                if sem is self.sem[k]:
                    continue
                if self.seen[k].get(id(sem), 0) >= val:
                    continue
                self.seen[k][id(sem)] = val
                waits.append((sem, val))
            self.ops[k].append((waits, None, None))

    def final_wait(self, eng, slots):
        waits = [(self.dma_sems[s][0], self.dma_sems[s][1]) for s in slots]
        self.ops[eng].append((waits, None, None))

    def emit(self):
        ops = self.ops

        def run(e, lst):
            for waits, fn, inc in lst:
                for sem, val in waits:
                    e.wait_ge(sem, val)
                if fn is not None:
                    fn(e).then_inc(inc[0], inc[1])

        with self.nc.Block() as block:
            @block.tensor
            def _(e):
                run(e, ops["pe"])

            @block.scalar
            def _(e):
                run(e, ops["act"])

            @block.vector
            def _(e):
                run(e, ops["dve"])

            @block.gpsimd
            def _(e):
                run(e, ops["pool"])

            @block.sync
            def _(e):
                run(e, ops["sp"])


XB_T0 = -656
H0_T0 = -640
H1_T0 = -384
H2_T0 = -256
XB_W, H0_W, H1_W, H2_W = 2336, 2304, 1792, 1536
PASSES = [
    ("conv", 0, -640, 1024), ("conv", 0, 384, 1024), ("conv", 0, 1408, 256),
    ("na", 1, -6, 7, {3: 1, 4: 2}, (4, 8), None),
    ("na", 1, 8, 7, {2: 3, 3: 4}, None, (6, 2)),
    ("conv", 2, -256, 1024), ("conv", 2, 768, 512),
    ("na", 3, 0, 8, {0: 1, 1: 2, 6: 3, 7: 4}, (1, 5), (10, 6)),
]
SRC_T0 = {0: XB_T0, 1: H0_T0, 2: H1_T0, 3: H2_T0}
DST_T0 = {0: H0_T0, 1: H1_T0, 2: H2_T0, 3: 0}


def even_tiles(n):
    k = (n + 511) // 512
    step = (n + k - 1) // k
    return tiles_of(n, step)


def tiles_of(n, step=512):
    out = []
    t = 0
    while t < n:
        out.append((t, min(step, n - t)))
        t += step
    return out


def build_fused():
    nc = bass.Bass("TRN2", target_bir_lowering=False)
    P = Planner(nc)

    def din(name, shape, dt=F32):
        return nc.dram_tensor(name, list(shape), dt, kind="ExternalInput").ap()

    xb = din("xb", [128, KC, XB_W])
    vecs = din("vecs", [4, 128, 16, KC])
    identd = din("ident", [128, 128])
    w_up = [din(f"w_up{i}", [32, 128, KC, 256]) for i in range(4)]
    w_dn = [din(f"w_dn{i}", [32, 128, KC, 256]) for i in range(4)]
    w_pw1 = {i: din(f"w_pw1_{i}", [16, 128, KC, 256]) for i in (0, 2)}
    w_pw2 = {i: din(f"w_pw2_{i}", [8, 128, KC, 256]) for i in (0, 2)}
    wdw = {i: din(f"wdw{i}", [128, KC, CW]) for i in (0, 2)}
    w_qkv = {i: din(f"w_qkv{i}", [16, 128, KC, 384]) for i in (1, 3)}
    w_o = {i: din(f"w_o{i}", [8, 128, KC, 256]) for i in (1, 3)}
    n_conv = sum(1 for p in PASSES if p[0] == "conv")
    n_na = sum(1 for p in PASSES if p[0] == "na")
    cmask = din("cmask", [n_conv, 128, 1056])
    nab = din("nab", [n_na, 16, 128, 5, 640])
    hout = nc.dram_tensor("hout", [128, KC, TOK], F32, kind="ExternalOutput").ap()
    scr = {0: xb,
           1: nc.dram_tensor("h0s", [128, KC, H0_W], F32).ap(),
           2: nc.dram_tensor("h1s", [128, KC, H1_W], F32).ap(),
           3: nc.dram_tensor("h2s", [128, KC, H2_W], F32).ap()}

    V_GMIX, V_GFFN, V_GFIN = 0, 1, 2
    V_BA, V_BG, V_BDW, V_LNG, V_LNB, V_BPW2 = 3, 4, 5, 6, 7, 8
    V_BQ, V_BK, V_BV, V_BO = 3, 4, 5, 6
    V_MTOP, V_NTOP, V_MBOT, V_NBOT = 12, 13, 14, 15

    SB_BASE = 16576
    off = [SB_BASE]

    def alloc(name, shape, dt):
        nbytes = int(np.prod(shape[1:])) * (4 if dt == F32 else 2)
        at = off[0]
        off[0] = at + ((nbytes + 63) // 64) * 64
        return nc.alloc_sbuf_tensor_at(name, list(shape), dt, offset=at), at

    NSLOT = 3
    ring = [alloc(f"ring{i}", [128, KC, 384], BF16)[0] for i in range(NSLOT)]
    r_ring = [Res(f"ring{i}") for i in range(NSLOT)]
    vt, _ = alloc("vecs_sb", [128, 16, KC], F32)
    ident, _ = alloc("ident_sb", [128, 128], BF16)
    onesD, _ = alloc("onesD", [128, 128], BF16)
    ones1, _ = alloc("ones1", [128, 128], BF16)
    epst, _ = alloc("epst", [128, 16], F32)
    sq = [alloc(f"sq{i}", [128, 512], BF16)[0] for i in range(2)]
    r_sq = [Res(f"sq{i}") for i in range(2)]
    rstd_t = [alloc(f"rstd{i}", [128, 512], F32)[0] for i in range(2)]
    r_rstd = [Res(f"rstd{i}") for i in range(2)]
    tmpa = [alloc(f"tmpa{i}", [128, 512], F32)[0] for i in range(2)]
    r_tmpa = [Res(f"tmpa{i}") for i in range(2)]
    base = off[0]
    r_vt, r_ident, r_ones = Res("vt"), Res("ident"), Res("ones")
    r_stg = Res("stg")
    r_H = [Res(f"H{j}") for j in range(KC)]
    r_A = [Res(f"A{j}") for j in range(KC)]
    r_B = [Res(f"B{j}") for j in range(KC)]

    off[0] = base
    cH, _ = alloc("cH", [128, KC, TOK], F32)
    cA, _ = alloc("cA", [128, KC, 1056], BF16)
    cB, _ = alloc("cB", [128, KC, 1056], BF16)
    cstg, _ = alloc("cstg", [128, KC, 32], F32)
    hh, _ = alloc("hh", [128, KC, 32], BF16)
    r_hh = Res("hh")
    wdw_sb, _ = alloc("wdw_sb", [128, KC, CW], F32)
    cm_sb, _ = alloc("cm_sb", [128, 1056], BF16)
    dg = [alloc(f"dg{i}", [128, CW, 128], BF16)[0] for i in range(2)]
    r_dg = [Res(f"dg{i}") for i in range(2)]
    mean_sb = [alloc(f"mean{i}", [128, 512], F32)[0] for i in range(2)]
    r_mean = [Res(f"mean{i}") for i in range(2)]
    _tb, _ = alloc("tmpb", [128, 512], F32)
    tmpb = [_tb, _tb]
    _rtb = Res("tmpb")
    r_tmpb = [_rtb, _rtb]
    r_wdw, r_cm = Res("wdw"), Res("cm")
    assert off[0] <= 229344, off[0]

    off[0] = base
    nA, a_at = alloc("nA", [128, KC, 1536], BF16)
    QT = [alloc(f"QT{i}", [128, TOK], BF16)[0] for i in range(2)]
    KT = [alloc(f"KT{i}", [128, 1536], BF16)[0] for i in range(2)]
    VT = [alloc(f"VT{i}", [128, 1536], BF16)[0] for i in range(2)]
    Vh = [alloc(f"Vh{i}", [128, 12, 128], BF16)[0] for i in range(2)]
    nb_sb = [alloc(f"nab{i}", [128, 5, 640], F32)[0] for i in range(2)]
    s_sb = [alloc(f"s_sb{i}", [128, 640], F32)[0] for i in range(2)]
    pt_sb = [alloc(f"pt_sb{i}", [128, 640], BF16)[0] for i in range(2)]
    rc_sb = [alloc(f"rc_sb{i}", [128, 128], F32)[0] for i in range(2)]
    bl_sb, _ = alloc("bl_sb", [128, 128], BF16)
    psm = [alloc(f"psm{i}", [128, 128], BF16)[0] for i in range(2)]
    r_psm = [Res(f"psm{i}") for i in range(2)]
    nstg2 = [alloc(f"nstg{i}", [128, KC, 128], F32)[0] for i in range(2)]
    r_nstg = [Res(f"nstg{i}") for i in range(2)]
    att_end = off[0]
    nB, _ = alloc("nB", [128, KC, TOK], BF16)
    nH = nc.alloc_sbuf_tensor_at("nH", [128, KC, TOK], F32, offset=a_at)
    nA2 = nc.alloc_sbuf_tensor_at("nA2", [128, KC, TOK], BF16, offset=a_at + KC * TOK * 4)
    assert a_at + KC * TOK * 6 <= att_end, (a_at, att_end)
    assert off[0] <= 229344, off[0]
    r_QT = [Res(f"QT{i}") for i in range(2)]
    r_KT = [Res(f"KT{i}") for i in range(2)]
    r_VT = [Res(f"VT{i}") for i in range(2)]
    r_Vh = [Res(f"Vh{i}") for i in range(2)]
    r_nb = [Res(f"nab{i}") for i in range(2)]
    r_s = [Res(f"s_sb{i}") for i in range(2)]
    r_pt = [Res(f"pt_sb{i}") for i in range(2)]
    r_rc = [Res(f"rc_sb{i}") for i in range(2)]
    r_bl = Res("bl")

    ps = [nc.alloc_psum_tensor(f"ps{i}", [128, 512], F32) for i in range(8)]
    r_ps = [Res(f"ps{i}") for i in range(8)]
    gb = [0]

    def next_bank(nb=3):
        b = gb[0] % nb
        gb[0] += 1
        return b

    P.dma("pool", lambda e: e.dma_start(out=ident[:], in_=identd), "ident", writes=[r_ident])
    P.op("dve", lambda e: e.memset(onesD[:], 1.0 / D), writes=[r_ones])
    P.op("dve", lambda e: e.memset(ones1[:], 1.0), writes=[r_ones])
    P.op("dve", lambda e: e.memset(epst[:], EPS), writes=[r_ones])

    def vec(slot, j):
        return vt[:, slot, j:j + 1]

    def rsqrt_to(i, src_ap, src_res, n):
        P.op("act", lambda e: e.activation(out=rstd_t[i][:, 0:n], in_=src_ap, func=AF.Sqrt, bias=epst[:, 0:1]),
             reads=src_res + [r_ones], writes=[r_rstd[i]])
        P.op("dve", lambda e: e.reciprocal(out=rstd_t[i][:, 0:n], in_=rstd_t[i][:, 0:n]),
             reads=[r_rstd[i]], writes=[r_rstd[i]])

    pc = [0]

    def load_piece(wd, p, fw):
        s = pc[0] % NSLOT
        pc[0] += 1
        P.dma("pool", lambda e: e.dma_start(out=ring[s][:, :, 0:fw], in_=wd[p]), f"ring{s}", writes=[r_ring[s]])
        return s

    def mm_group(bank, n, lhs_fn, rhs_fn, nk, reads):
        def f(e):
            ins = None
            for k in range(nk):
                ins = e.matmul(ps[bank][:, 0:n], lhs_fn(k), rhs_fn(k), start=(k == 0), stop=(k == nk - 1))
            return ins
        P.op("pe", f, reads=reads, writes=[r_ps[bank]])

    nrm = [0]

    def rms_tile(src_fn, src_res, n, gslot, dst_fn, dst_res_fn):
        i = nrm[0] % 2
        nrm[0] += 1
        bank = 3 + (nrm[0] % 2)
        for j in range(KC):
            q = j % 2
            P.op("act", lambda e, j=j, q=q: e.activation(out=sq[q][:, 0:n], in_=src_fn(j), func=AF.Square),
                 reads=src_res(j), writes=[r_sq[q]])
            P.op("pe", lambda e, j=j, q=q: e.matmul(ps[bank][:, 0:n], onesD[:], sq[q][:, 0:n],
                                                    start=(j == 0), stop=(j == KC - 1)),
                 reads=[r_sq[q], r_ones], writes=[r_ps[bank]] if j == 0 else [])
        r_ps[bank].w = ("pe", P.sem["pe"], P.cnt["pe"])
        rsqrt_to(i, ps[bank][:, 0:n], [r_ps[bank]], n)
        for j in range(KC):
            P.op("dve", lambda e, j=j: e.scalar_tensor_tensor(out=dst_fn(j), in0=src_fn(j), scalar=vec(gslot, j),
                                                              in1=rstd_t[i][:, 0:n], op0=ALU.mult, op1=ALU.mult),
                 reads=src_res(j) + [r_rstd[i], r_vt], writes=dst_res_fn(j))

    def mlp(li, H, HN, r_HN, UP, r_UP, tts):
        for (t0, n) in tts:
            rms_tile(lambda j, t0=t0, n=n: H[:, j, t0:t0 + n], lambda j: [r_H[j]], n, V_GFFN,
                     lambda j, t0=t0, n=n: HN[:, j, t0:t0 + n], lambda j: [r_HN[j]])
        for blk in range(4):
            for pp in range(8):
                s = load_piece(w_up[li], blk * 8 + pp, 256)
                for fc in range(2):
                    c = 2 * pp + fc
                    for (t0, n) in tts:
                        bank = next_bank()
                        mm_group(bank, n, lambda k, s=s, fc=fc: ring[s][:, k, fc * 128:(fc + 1) * 128],
                                 lambda k, t0=t0, n=n: HN[:, k, t0:t0 + n], KC, [r_ring[s]] + r_HN)
                        q = gb[0] % 2
                        P.op("act", lambda e, bank=bank, q=q, n=n: e.activation(out=tmpa[q][:, 0:n], in_=ps[bank][:, 0:n],
                                                                                func=AF.Relu),
                             reads=[r_ps[bank]], writes=[r_tmpa[q]])
                        P.op("dve", lambda e, q=q, c=c, t0=t0, n=n: e.tensor_tensor(
                            out=UP[:, c, t0:t0 + n], in0=tmpa[q][:, 0:n], in1=tmpa[q][:, 0:n], op=ALU.mult),
                            reads=[r_tmpa[q]], writes=[r_UP[c]])
            for pp in range(8):
                s = load_piece(w_dn[li], blk * 8 + pp, 256)
                for fc in range(2):
                    j = 2 * pp + fc
                    for (t0, n) in tts:
                        bank = next_bank()
                        mm_group(bank, n, lambda k, s=s, fc=fc: ring[s][:, k, fc * 128:(fc + 1) * 128],
                                 lambda k, t0=t0, n=n: UP[:, k, t0:t0 + n], KC, [r_ring[s]] + r_UP)
                        P.op("dve", lambda e, bank=bank, j=j, t0=t0, n=n: e.tensor_tensor(
                            out=H[:, j, t0:t0 + n], in0=ps[bank][:, 0:n], in1=H[:, j, t0:t0 + n], op=ALU.add),
                            reads=[r_ps[bank], r_H[j]], writes=[r_H[j]])

    def store_out(H, dst, d0, n, tts, final, tag):
        if final:
            for (t0, tn) in tts:
                i = nrm[0] % 2
                nrm[0] += 1
                bank = 3 + (nrm[0] % 2)
                for j in range(KC):
                    q = j % 2
                    P.op("act", lambda e, j=j, q=q, t0=t0, tn=tn: e.activation(out=sq[q][:, 0:tn], in_=H[:, j, t0:t0 + tn],
                                                                               func=AF.Square),
                         reads=[r_H[j]], writes=[r_sq[q]])
                    P.op("pe", lambda e, j=j, q=q, bank=bank, tn=tn: e.matmul(ps[bank][:, 0:tn], onesD[:], sq[q][:, 0:tn],
                                                                              start=(j == 0), stop=(j == KC - 1)),
                         reads=[r_sq[q], r_ones], writes=[r_ps[bank]] if j == 0 else [])
                r_ps[bank].w = ("pe", P.sem["pe"], P.cnt["pe"])
                rsqrt_to(i, ps[bank][:, 0:tn], [r_ps[bank]], tn)
                for j in range(KC):
                    P.op("dve", lambda e, j=j, i=i, t0=t0, tn=tn: e.scalar_tensor_tensor(
                        out=H[:, j, t0:t0 + tn], in0=H[:, j, t0:t0 + tn], scalar=vec(V_GFIN, j),
                        in1=rstd_t[i][:, 0:tn], op0=ALU.mult, op1=ALU.mult),
                        reads=[r_H[j], r_rstd[i], r_vt], writes=[r_H[j]])
        slots = []
        for g in range(4):
            P.dma("sp", lambda e, g=g: e.dma_start(out=dst[:, 4 * g:4 * g + 4, d0:d0 + n], in_=H[:, 4 * g:4 * g + 4, 0:n]),
                  f"out{g}", reads=r_H[4 * g:4 * g + 4])
            slots.append(f"out{g}")
        return slots

    def conv_pass(li, a0, n, ci):
        H, A, B = cH, cA, cB
        src = scr[li]
        dst = scr[li + 1]
        s0 = a0 - 16 - SRC_T0[li]
        d0 = a0 - DST_T0[li]
        NB = n + 32
        tts = tiles_of(n)
        P.dma("sp", lambda e: e.dma_start(out=vt[:], in_=vecs[li]), "vt", writes=[r_vt])
        for g in range(4):
            P.dma("sp", lambda e, g=g: e.dma_start(out=H[:, 4 * g:4 * g + 4, 0:n], in_=src[:, 4 * g:4 * g + 4, s0 + 16:s0 + 16 + n]),
                  f"hin{g}", writes=r_H[4 * g:4 * g + 4])
        P.dma("sp", lambda e: e.dma_start(out=cstg[:, :, 0:16], in_=src[:, :, s0:s0 + 16]), "stg", writes=[r_stg])
        P.dma("sp", lambda e: e.dma_start(out=cstg[:, :, 16:32], in_=src[:, :, s0 + 16 + n:s0 + 32 + n]), "stg2", writes=[r_stg])
        P.dma("sp", lambda e: e.dma_start(out=wdw_sb[:], in_=wdw[li]), "wdw", writes=[r_wdw])
        P.dma("pool", lambda e: e.dma_start(out=cm_sb[:], in_=cmask[ci]), "cm", writes=[r_cm])
        for (t0, tn) in tts:
            rms_tile(lambda j, t0=t0, tn=tn: H[:, j, t0:t0 + tn], lambda j: [r_H[j]], tn, V_GMIX,
                     lambda j, t0=t0, tn=tn: A[:, j, 16 + t0:16 + t0 + tn], lambda j: [r_A[j]])
        rms_tile(lambda j: cstg[:, j, 0:32], lambda j: [r_stg], 32, V_GMIX, lambda j: hh[:, j, :], lambda j: [r_hh])
        for j in range(KC):
            P.op("act", lambda e, j=j: e.activation(out=A[:, j, 0:16], in_=hh[:, j, 0:16], func=AF.Identity),
                 reads=[r_hh], writes=[r_A[j]])
            P.op("act", lambda e, j=j: e.activation(out=A[:, j, 16 + n:NB], in_=hh[:, j, 16:32], func=AF.Identity),
                 reads=[r_hh], writes=[r_A[j]])
        for p in range(16):
            s = load_piece(w_pw1[li], p, 256)
            for (t0, tn) in even_tiles(NB):
                banks = []
                for fc in range(2):
                    bank = next_bank()
                    banks.append(bank)
                    mm_group(bank, tn, lambda k, s=s, fc=fc: ring[s][:, k, fc * 128:(fc + 1) * 128],
                             lambda k, t0=t0, tn=tn: A[:, k, t0:t0 + tn], KC, [r_ring[s]] + r_A)
                q = gb[0] % 2
                ba, bg = banks
                P.op("act", lambda e, bg=bg, q=q, tn=tn, p=p: e.activation(out=tmpa[q][:, 0:tn], in_=ps[bg][:, 0:tn],
                                                                           func=AF.Sigmoid, bias=vec(V_BG, p)),
                     reads=[r_ps[bg], r_vt], writes=[r_tmpa[q]])
                P.op("dve", lambda e, ba=ba, q=q, tn=tn, p=p, t0=t0: e.scalar_tensor_tensor(
                    out=B[:, p, t0:t0 + tn], in0=ps[ba][:, 0:tn], scalar=vec(V_BA, p), in1=tmpa[q][:, 0:tn],
                    op0=ALU.add, op1=ALU.mult), reads=[r_ps[ba], r_tmpa[q], r_vt], writes=[r_B[p]])
        for j in range(KC):
            P.op("dve", lambda e, j=j: e.tensor_tensor(out=B[:, j, 0:NB], in0=B[:, j, 0:NB], in1=cm_sb[:, 0:NB], op=ALU.mult),
                 reads=[r_B[j], r_cm], writes=[r_B[j]])
        r_Y = [[Res(f"Y{ti}_{j}") for j in range(KC)] for ti in range(len(tts))]
        r_Z = [[Res(f"Z{ti}_{j}") for j in range(KC)] for ti in range(len(tts))]
        for ti, (t0, tn) in enumerate(tts):
            for j in range(KC):
                di = (ti * KC + j) % 2
                P.op("dve", lambda e, j=j, di=di: e.tensor_tensor(
                    out=dg[di][:], in0=ident[:].unsqueeze(1).broadcast_to([128, CW, 128]),
                    in1=wdw_sb[:, j, :].unsqueeze(2).broadcast_to([128, CW, 128]), op=ALU.mult),
                    reads=[r_ident, r_wdw], writes=[r_dg[di]])
                bank = next_bank()
                mm_group(bank, tn, lambda k, di=di: dg[di][:, k, :],
                         lambda k, j=j, t0=t0, tn=tn: B[:, j, t0 + k + 1:t0 + k + 1 + tn], CW, [r_dg[di], r_B[j]])
                P.op("act", lambda e, bank=bank, j=j, t0=t0, tn=tn: e.activation(out=A[:, j, t0:t0 + tn], in_=ps[bank][:, 0:tn],
                                                                                 func=AF.Identity, bias=vec(V_BDW, j)),
                     reads=[r_ps[bank], r_vt], writes=[r_Y[ti][j]], after=[r_A[j]])
            i = ti % 2
            bm, bq = 4 + 2 * i, 5 + 2 * i
            for j in range(KC):
                q = j % 2
                P.op("act", lambda e, j=j, q=q, t0=t0, tn=tn: e.activation(out=sq[q][:, 0:tn], in_=A[:, j, t0:t0 + tn], func=AF.Square),
                     reads=[r_Y[ti][j]], writes=[r_sq[q]])
                P.op("pe", lambda e, j=j, t0=t0, tn=tn, bm=bm: e.matmul(ps[bm][:, 0:tn], onesD[:], A[:, j, t0:t0 + tn],
                                                                        start=(j == 0), stop=(j == KC - 1)),
                     reads=[r_Y[ti][j], r_ones], writes=[r_ps[bm]] if j == 0 else [])
                P.op("pe", lambda e, j=j, q=q, bq=bq, tn=tn: e.matmul(ps[bq][:, 0:tn], onesD[:], sq[q][:, 0:tn],
                                                                      start=(j == 0), stop=(j == KC - 1)),
                     reads=[r_sq[q], r_ones], writes=[r_ps[bq]] if j == 0 else [])
            r_ps[bm].w = ("pe", P.sem["pe"], P.cnt["pe"])
            r_ps[bq].w = ("pe", P.sem["pe"], P.cnt["pe"])
            P.op("act", lambda e, i=i, bm=bm, tn=tn: e.activation(out=mean_sb[i][:, 0:tn], in_=ps[bm][:, 0:tn], func=AF.Identity),
                 reads=[r_ps[bm]], writes=[r_mean[i]])
            P.op("dve", lambda e, i=i, tn=tn: e.tensor_tensor(out=tmpb[i][:, 0:tn], in0=mean_sb[i][:, 0:tn], in1=mean_sb[i][:, 0:tn],
                                                              op=ALU.mult), reads=[r_mean[i]], writes=[r_tmpb[i]])
            P.op("dve", lambda e, i=i, bq=bq, tn=tn: e.tensor_tensor(out=tmpb[i][:, 0:tn], in0=ps[bq][:, 0:tn], in1=tmpb[i][:, 0:tn],
                                                                     op=ALU.subtract),
                 reads=[r_ps[bq], r_tmpb[i]], writes=[r_tmpb[i]])
            rsqrt_to(i, tmpb[i][:, 0:tn], [r_tmpb[i]], tn)
            for j in range(KC):
                q = j % 2
                P.op("dve", lambda e, j=j, q=q, i=i, t0=t0, tn=tn: e.tensor_tensor(out=tmpa[q][:, 0:tn], in0=A[:, j, t0:t0 + tn],
                                                                                   in1=mean_sb[i][:, 0:tn], op=ALU.subtract),
                     reads=[r_Y[ti][j], r_mean[i]], writes=[r_tmpa[q]])
                P.op("dve", lambda e, j=j, q=q, i=i, tn=tn: e.scalar_tensor_tensor(out=tmpa[q][:, 0:tn], in0=tmpa[q][:, 0:tn],
                                                                                   scalar=vec(V_LNG, j), in1=rstd_t[i][:, 0:tn],
                                                                                   op0=ALU.mult, op1=ALU.mult),
                     reads=[r_tmpa[q], r_rstd[i], r_vt], writes=[r_tmpa[q]])
                P.op("act", lambda e, j=j, q=q, t0=t0, tn=tn: e.activation(out=B[:, j, t0:t0 + tn], in_=tmpa[q][:, 0:tn],
                                                                           func=AF.Silu, bias=vec(V_LNB, j)),
                     reads=[r_tmpa[q], r_vt], writes=[r_Z[ti][j]], after=[r_B[j]])
        for ti, (t0, tn) in enumerate(tts):
            for p in range(8):
                s = load_piece(w_pw2[li], p, 256)
                for fc in range(2):
                    j = 2 * p + fc
                    bank = next_bank()
                    mm_group(bank, tn, lambda k, s=s, fc=fc: ring[s][:, k, fc * 128:(fc + 1) * 128],
                             lambda k, t0=t0, tn=tn: B[:, k, t0:t0 + tn], KC, [r_ring[s]] + r_Z[ti] + r_B)
                    P.op("dve", lambda e, bank=bank, j=j, t0=t0, tn=tn: e.scalar_tensor_tensor(
                        out=H[:, j, t0:t0 + tn], in0=ps[bank][:, 0:tn], scalar=vec(V_BPW2, j),
                        in1=H[:, j, t0:t0 + tn], op0=ALU.add, op1=ALU.add),
                        reads=[r_ps[bank], r_H[j], r_vt], writes=[r_H[j]])
        for ti in range(len(tts)):
            for j in range(KC):
                r_A[j].readers.extend(r_Y[ti][j].readers)
        mlp(li, H, A, r_A, B, r_B, tts)
        return store_out(H, dst, d0, n, tts, False, f"c{ci}")

    def na_pass(li, first_row, nblk, varmap, topfix, botfix, ni):
        A, B, H, A2 = nA, nB, nH, nA2
        src = scr[li]
        final = li == 3
        dst = hout if final else scr[li + 1]
        n = nblk * 128
        ntile = nblk + 4
        NB = ntile * 128
        s0 = first_row * 64 - 256 - SRC_T0[li]
        d0 = first_row * 64 - DST_T0[li]
        tts = tiles_of(n)
        P.dma("sp", lambda e: e.dma_start(out=vt[:], in_=vecs[li]), "vt", writes=[r_vt])
        for t in range(ntile):
            sb = t % 2
            P.dma("sp", lambda e, t=t, sb=sb: e.dma_start(out=nstg2[sb][:], in_=src[:, :, s0 + t * 128:s0 + (t + 1) * 128]),
                  f"nstg{sb}", writes=[r_nstg[sb]])
            rms_tile(lambda j, sb=sb: nstg2[sb][:, j, :], lambda j, sb=sb: [r_nstg[sb]], 128, V_GMIX,
                     lambda j, t=t: A[:, j, t * 128:(t + 1) * 128], lambda j: [r_A[j]])
        pT = ps[3][:].bitcast(BF16)

        def gemm_steps(h):
            hb = h % 2
            hold = {}
            steps = []

            def first():
                P.dma("sp", lambda e: e.dma_start(out=nb_sb[hb][:], in_=nab[ni, h]), f"nab{hb}", writes=[r_nb[hb]])
                hold["s"] = load_piece(w_qkv[li], h, 384)
            steps.append(first)

            def qstep(t0, tn):
                s = hold["s"]
                bank = next_bank()
                mm_group(bank, tn, lambda k: ring[s][:, k, 0:128],
                         lambda k: A[:, k, 256 + t0:256 + t0 + tn], KC, [r_ring[s]] + r_A)
                P.op("dve", lambda e: e.tensor_scalar(
                    out=QT[hb][:, t0:t0 + tn], in0=ps[bank][:, 0:tn], scalar1=vec(V_BQ, h), scalar2=128 ** -0.5,
                    op0=ALU.add, op1=ALU.mult), reads=[r_ps[bank], r_vt], writes=[r_QT[hb]])

            def kvstep(fc, dstb, r_dst, vs, t0, tn):
                s = hold["s"]
                bank = next_bank()
                mm_group(bank, tn, lambda k: ring[s][:, k, fc * 128:(fc + 1) * 128],
                         lambda k: A[:, k, t0:t0 + tn], KC, [r_ring[s]] + r_A)
                P.op("act", lambda e: e.activation(
                    out=dstb[hb][:, t0:t0 + tn], in_=ps[bank][:, 0:tn], func=AF.Identity, bias=vec(vs, h)),
                    reads=[r_ps[bank], r_vt], writes=[r_dst[hb]])

            def trstep(g0, cnt):
                def tr(e):
                    ins = None
                    for i in range(cnt):
                        kt = g0 + i
                        ins = e.transpose(pT[:, i * 128:(i + 1) * 128], VT[hb][:, kt * 128:(kt + 1) * 128], ident[:])
                    return ins
                P.op("pe", tr, reads=[r_VT[hb], r_ident], writes=[r_ps[3]])
                P.op("act", lambda e: e.activation(
                    out=Vh[hb][:, g0:g0 + cnt, :].rearrange("p a b -> p (a b)"), in_=pT[:, 0:cnt * 128], func=AF.Identity),
                    reads=[r_ps[3]], writes=[r_Vh[hb]])

            def fixstep():
                for fix, vm, vn in ((topfix, V_MTOP, V_NTOP), (botfix, V_MBOT, V_NBOT)):
                    if fix is None:
                        continue
                    dt_, st_ = fix
                    for (rbuf, d_ap, s_ap) in (
                            (r_KT, KT[hb][:, dt_ * 128:(dt_ + 1) * 128], KT[hb][:, st_ * 128:(st_ + 1) * 128]),
                            (r_Vh, Vh[hb][:, dt_, :], Vh[hb][:, st_, :])):
                        P.op("dve", lambda e, s_ap=s_ap, vm=vm: e.tensor_scalar(
                            out=bl_sb[:], in0=s_ap, scalar1=vec(vm, 0), scalar2=None, op0=ALU.mult),
                            reads=[rbuf[hb], r_vt], writes=[r_bl])
                        P.op("dve", lambda e, d_ap=d_ap, vn=vn: e.scalar_tensor_tensor(
                            out=d_ap, in0=d_ap, scalar=vec(vn, 0), in1=bl_sb[:], op0=ALU.mult, op1=ALU.add),
                            reads=[rbuf[hb], r_bl, r_vt], writes=[rbuf[hb]])

            for (t0, tn) in tts:
                steps.append(lambda t0=t0, tn=tn: qstep(t0, tn))
            for (fc, dstb, r_dst, vs) in ((1, KT, r_KT, V_BK), (2, VT, r_VT, V_BV)):
                for (t0, tn) in tiles_of(NB):
                    steps.append(lambda fc=fc, dstb=dstb, r_dst=r_dst, vs=vs, t0=t0, tn=tn: kvstep(fc, dstb, r_dst, vs, t0, tn))
            for g0 in range(0, ntile, 4):
                steps.append(lambda g0=g0: trstep(g0, min(4, ntile - g0)))
            steps.append(fixstep)
            return steps

        def qk_stage(idx):
            h, b = divmod(idx, nblk)
            hb = h % 2
            v = varmap.get(b, 0)
            ab = idx % 2
            X, Y = 4 + 2 * ab, 5 + 2 * ab

            def qk(e):
                ins = None
                for i in range(5):
                    o = ps[X][:, i * 128:(i + 1) * 128] if i < 4 else ps[Y][:, 0:128]
                    ins = e.matmul(o, KT[hb][:, (b + i) * 128:(b + i + 1) * 128], QT[hb][:, b * 128:(b + 1) * 128],
                                   start=True, stop=True)
                return ins
            P.op("pe", qk, reads=[r_KT[hb], r_QT[hb]], writes=[r_ps[X], r_ps[Y]])
            P.op("dve", lambda e: e.scalar_tensor_tensor(
                out=s_sb[ab][:, 0:512], in0=ps[X][:], scalar=80.0, in1=nb_sb[hb][:, v, 0:512], op0=ALU.min, op1=ALU.add),
                reads=[r_ps[X], r_nb[hb]], writes=[r_s[ab]])
            P.op("dve", lambda e: e.scalar_tensor_tensor(
                out=s_sb[ab][:, 512:640], in0=ps[Y][:, 0:128], scalar=80.0, in1=nb_sb[hb][:, v, 512:640],
                op0=ALU.min, op1=ALU.add), reads=[r_ps[Y], r_nb[hb]], writes=[r_s[ab]])
            P.op("act", lambda e: e.activation(out=pt_sb[ab][:], in_=s_sb[ab][:], func=AF.Exp),
                 reads=[r_s[ab]], writes=[r_pt[ab]])

        def pv_stage(idx):
            h, b = divmod(idx, nblk)
            hb = h % 2
            ab = idx % 2
            Y = 5 + 2 * ab

            def pv(e):
                ins = None
                for i in range(5):
                    ins = e.matmul(ps[Y][:, 128:256], Vh[hb][:, b + i, :], pt_sb[ab][:, i * 128:(i + 1) * 128],
                                   start=(i == 0), stop=(i == 4))
                for i in range(5):
                    ins = e.matmul(ps[Y][:, 256:384], ones1[:], pt_sb[ab][:, i * 128:(i + 1) * 128],
                                   start=(i == 0), stop=(i == 4))
                return ins
            P.op("pe", pv, reads=[r_Vh[hb], r_pt[ab], r_ones], writes=[r_ps[Y]])
            P.op("dve", lambda e: e.reciprocal(out=rc_sb[ab][:], in_=ps[Y][:, 256:384]),
                 reads=[r_ps[Y]], writes=[r_rc[ab]])
            P.op("dve", lambda e: e.tensor_tensor(
                out=B[:, h, b * 128:(b + 1) * 128], in0=ps[Y][:, 128:256], in1=rc_sb[ab][:], op=ALU.mult),
                reads=[r_ps[Y], r_rc[ab]], writes=[r_B[h]])

        for st in gemm_steps(0):
            st()
        pending = []
        nidx = 16 * nblk
        qk_stage(0)
        for idx in range(nidx):
            h, b = divmod(idx, nblk)
            if b == 0 and h + 1 < 16:
                pending = gemm_steps(h + 1)
            if b < nblk - 1:
                if idx + 1 < nidx:
                    qk_stage(idx + 1)
                left = nblk - 1 - b
                take = (len(pending) + left - 1) // left
                for st in pending[:take]:
                    st()
                pending = pending[take:]
            else:
                for st in pending:
                    st()
                pending = []
                if idx + 1 < nidx:
                    qk_stage(idx + 1)
            pv_stage(idx)
        dead = r_A + r_QT + r_KT + r_VT + r_Vh + r_nb + r_s + r_pt + r_rc + r_nstg + r_psm + [r_bl]
        for g in range(4):
            P.dma("sp", lambda e, g=g: e.dma_start(out=H[:, 4 * g:4 * g + 4, 0:n], in_=src[:, 4 * g:4 * g + 4, s0 + 256:s0 + 256 + n]),
                  f"hin{g}", writes=r_H[4 * g:4 * g + 4] + dead)
        for p in range(8):
            s = load_piece(w_o[li], p, 256)
            for fc in range(2):
                j = 2 * p + fc
                for (t0, tn) in tts:
                    bank = next_bank()
                    mm_group(bank, tn, lambda k, s=s, fc=fc: ring[s][:, k, fc * 128:(fc + 1) * 128],
                             lambda k, t0=t0, tn=tn: B[:, k, t0:t0 + tn], KC, [r_ring[s]] + r_B)
                    P.op("dve", lambda e, bank=bank, j=j, t0=t0, tn=tn: e.scalar_tensor_tensor(
                        out=H[:, j, t0:t0 + tn], in0=ps[bank][:, 0:tn], scalar=vec(V_BO, j),
                        in1=H[:, j, t0:t0 + tn], op0=ALU.add, op1=ALU.add),
                        reads=[r_ps[bank], r_H[j], r_vt], writes=[r_H[j]])
        r_A2 = [Res(f"A2_{ni}_{j}") for j in range(KC)]
        mlp(li, H, A2, r_A2, B, r_B, tts)
        return store_out(H, dst, d0, n, tts, final, f"n{ni}")

    ci = ni = 0
    slots = []
    for ps_ in PASSES:
        if ps_[0] == "conv":
            slots = conv_pass(ps_[1], ps_[2], ps_[3], ci)
            ci += 1
        else:
            slots = na_pass(ps_[1], ps_[2], ps_[3], ps_[4], ps_[5], ps_[6], ni)
            ni += 1
        P.barrier()
    P.final_wait("sp", slots)
    P.emit()
    return nc


def fm(x):
    t = x.shape[0]
    return np.ascontiguousarray(x.T.reshape(KC, 128, t).transpose(1, 0, 2))


def unfm(y):
    t = y.shape[2]
    return np.ascontiguousarray(y.transpose(1, 0, 2).reshape(D, t).T)


def pieces(w, fw):
    f = w.shape[1]
    return np.ascontiguousarray(w.reshape(KC, 128, f // fw, fw).transpose(2, 1, 0, 3))


def vec16(v):
    return np.ascontiguousarray(v.reshape(KC, 128).T)


def na_tables(rpb, c, first_row, nblk, varmap, topfix, botfix):
    nrow = 2 * (nblk + 4)
    srow = np.full(nrow, -1, np.int64)
    for s in range(nrow):
        g = 16 * c + first_row - 4 + s
        if 0 <= g < 128:
            srow[s] = g
    if topfix is not None and c == 0:
        srow[2 * topfix[0]:2 * topfix[0] + 2] = srow[2 * topfix[1]:2 * topfix[1] + 2]
    if botfix is not None and c == NCORE - 1:
        srow[2 * botfix[0]:2 * botfix[0] + 2] = srow[2 * botfix[1]:2 * botfix[1] + 2]
    kc = np.arange(64)[:, None]
    qc = np.arange(64)[None, :]
    cs = np.clip(qc - 8, 0, 48)
    colok = (kc >= cs) & (kc < cs + 16)
    cidx = np.clip(kc - qc + 15, 0, 30)
    out = np.full((16, 128, 5, 640), NEG, np.float32)

    def put(var, u, a, ri):
        i, s2 = u // 2, u % 2
        blk = np.where(colok[None], rpb[:, ri][:, cidx], np.float32(NEG))
        out[:, s2 * 64:(s2 + 1) * 64, var, i * 128 + a * 64:i * 128 + (a + 1) * 64] = blk

    for a in range(2):
        for u in range(10):
            dlt = u - 4 - a
            if -4 <= dlt <= 3:
                put(0, u, a, dlt + 7)
    for b, var in varmap.items():
        for a in range(2):
            r = 16 * c + first_row + 2 * b + a
            if not (0 <= r < 128) or (c not in (0, NCORE - 1)):
                for u in range(10):
                    dlt = u - 4 - a
                    if -4 <= dlt <= 3:
                        put(var, u, a, dlt + 7)
                continue
            rs = min(max(r - 4, 0), 120)
            used = set()
            natural = lambda u: srow[2 * b + u] == 16 * c + first_row - 4 + 2 * b + u
            order = sorted(range(10), key=lambda u: (not natural(u), u))
            for u in order:
                kr = srow[2 * b + u]
                if kr < rs or kr >= rs + 8 or kr in used:
                    continue
                used.add(kr)
                put(var, u, a, kr - r + 7)
            assert len(used) == 8, (c, first_row, b, a, sorted(used))
    return out


_NC = []


def kernel(x, norm_mix_g, norm_ffn_g, final_norm_g,
           conv_w_pw1, conv_b_pw1, conv_w_dw, conv_b_dw, conv_ln_g, conv_ln_b,
           conv_w_pw2, conv_b_pw2,
           na_w_qkv, na_b_qkv, na_rpb, na_w_o, na_b_o,
           ffn_w_up, ffn_w_down):
    f = lambda a: np.asarray(a, np.float32)
    xs = f(x)[0]
    if not _NC:
        _NC.append(build_fused())
    nc = _NC[0]
    T = NCORE * TOK
    common = {"ident": np.eye(128, dtype=np.float32)}
    vecs = np.zeros((4, 128, 16, KC), np.float32)
    for i in range(4):
        j = i // 2
        vecs[i, :, 0] = vec16(f(norm_mix_g)[i])
        vecs[i, :, 1] = vec16(f(norm_ffn_g)[i])
        vecs[i, :, 2] = vec16(f(final_norm_g))
        common[f"w_up{i}"] = pieces(f(ffn_w_up)[i], 256)
        common[f"w_dn{i}"] = np.ascontiguousarray(
            f(ffn_w_down)[i].reshape(4, KC, 128, 8, 256).transpose(0, 3, 2, 1, 4).reshape(32, 128, KC, 256))
        if i % 2 == 0:
            w1 = f(conv_w_pw1)[j]
            w1p = np.concatenate([w1[:, :D].reshape(D, KC, 1, 128), w1[:, D:].reshape(D, KC, 1, 128)], axis=2).reshape(D, 2 * D)
            b1 = f(conv_b_pw1)[j]
            vecs[i, :, 3] = vec16(b1[:D])
            vecs[i, :, 4] = vec16(b1[D:])
            vecs[i, :, 5] = vec16(f(conv_b_dw)[j])
            vecs[i, :, 6] = vec16(f(conv_ln_g)[j])
            vecs[i, :, 7] = vec16(f(conv_ln_b)[j])
            vecs[i, :, 8] = vec16(f(conv_b_pw2)[j])
            common[f"w_pw1_{i}"] = pieces(w1p, 256)
            common[f"w_pw2_{i}"] = pieces(f(conv_w_pw2)[j], 256)
            common[f"wdw{i}"] = np.ascontiguousarray(f(conv_w_dw)[j].T.reshape(KC, 128, CW).transpose(1, 0, 2))
        else:
            wq = f(na_w_qkv)[j]
            wqp = np.concatenate([wq[:, 0:D].reshape(D, 16, 1, 128), wq[:, D:2 * D].reshape(D, 16, 1, 128),
                                  wq[:, 2 * D:].reshape(D, 16, 1, 128)], axis=2).reshape(D, 3 * D)
            bq = f(na_b_qkv)[j]
            vecs[i, :, 3] = vec16(bq[0:D])
            vecs[i, :, 4] = vec16(bq[D:2 * D])
            vecs[i, :, 5] = vec16(bq[2 * D:])
            vecs[i, :, 6] = vec16(f(na_b_o)[j])
            common[f"w_qkv{i}"] = pieces(wqp, 384)
            common[f"w_o{i}"] = pieces(f(na_w_o)[j], 256)
    conv_passes = [p for p in PASSES if p[0] == "conv"]
    na_passes = [p for p in PASSES if p[0] == "na"]
    tab_cache = {}
    in_maps = []
    for c in range(NCORE):
        m = dict(common)
        g = np.arange(XB_T0, XB_T0 + XB_W) + c * TOK
        ok = (g >= 0) & (g < T)
        xbnd = np.zeros((XB_W, D), np.float32)
        xbnd[ok] = xs[g[ok]]
        m["xb"] = fm(xbnd)
        vc = vecs.copy()
        vc[:, :, 12] = 1.0 if c == 0 else 0.0
        vc[:, :, 13] = 0.0 if c == 0 else 1.0
        vc[:, :, 14] = 1.0 if c == NCORE - 1 else 0.0
        vc[:, :, 15] = 0.0 if c == NCORE - 1 else 1.0
        m["vecs"] = vc
        cm = np.zeros((len(conv_passes), 128, 1056), np.float32)
        for pi, (_, li, a0, n) in enumerate(conv_passes):
            gg = c * TOK + a0 - 16 + np.arange(n + 32)
            cm[pi, :, :n + 32] = ((gg >= 0) & (gg < T)).astype(np.float32)[None, :]
        m["cmask"] = cm
        key = 0 if c == 0 else (2 if c == NCORE - 1 else 1)
        if key not in tab_cache:
            tab_cache[key] = np.stack([na_tables(f(na_rpb)[p[1] // 2], c, p[2], p[3], p[4], p[5], p[6]) for p in na_passes])
        m["nab"] = tab_cache[key]
        in_maps.append(m)
    res = run_bass_kernel_spmd(nc, in_maps, core_ids=list(range(NCORE)))
    out = np.concatenate([unfm(res.results[c]["hout"]) for c in range(NCORE)], axis=0)
    return out[None].astype(np.float32)
```
